# Optimizing a Trainium2 kernel written in Bass

```python
import math
import jax, jax.numpy as jnp
from jax import lax
import numpy as np

D_MODEL = 2048
BATCH = 4
SEQ = 4096
DEPTH = 1
DEC_BATCH = 2
DEC_SEQ = 8192
PAST_LEN = 128

P_DIM = 256
RET_HEADS = 8
RET_DK = 128
RET_DV = 256
RET_CHUNK = 128
ROPE_BASE = 10000.0
ATT_GROUPS = ((128, 1), (512, 4), (2048, 16))
ATT_HEADS = 8
ATT_DH = 128
N_ATT_HEADS = ATT_HEADS * len(ATT_GROUPS)
NEG_INF = -1e30
NUM_BUCKETS = 32
REL_MAX_DIST = 1024
PEER_HEADS = 8
PEER_NKEYS = 128
PEER_N = PEER_NKEYS * PEER_NKEYS
PEER_QDIM = 256
PEER_QHALF = PEER_QDIM // 2
PEER_TOPK = 16
PEER_BLOCK = 64
DN_ALPHA = (2.0 * DEPTH) ** 0.25
DN_BETA = (8.0 * DEPTH) ** -0.25
LN_EPS = 1e-5

RET_QK_W = RET_HEADS * RET_DK
RET_V_W = RET_HEADS * RET_DV
ATT_W = N_ATT_HEADS * ATT_DH
ATT_OUT_W = ATT_HEADS * ATT_DH
SPLITS = (RET_QK_W, RET_QK_W, RET_V_W, RET_V_W, ATT_W, ATT_W, ATT_W, D_MODEL, D_MODEL)
SPLIT_IDX = tuple(int(c) for c in np.cumsum(SPLITS)[:-1])
IN_W = int(sum(SPLITS))

kernel_name = "hybrid_retention_dilated_peer_encoder"

f32 = jnp.float32


def _layernorm(x, g, b):
    xf = x.astype(f32)
    mu = xf.mean(-1, keepdims=True)
    var = jnp.square(xf - mu).mean(-1, keepdims=True)
    y = (xf - mu) * lax.rsqrt(var + LN_EPS) * g.astype(f32) + b.astype(f32)
    return y.astype(x.dtype)


def _rotary(t):
    S = t.shape[1]
    half = t.shape[-1] // 2
    inv = 1.0 / (ROPE_BASE ** jnp.linspace(0.0, 1.0, half, dtype=f32))
    ang = jnp.arange(S, dtype=f32)[:, None] * inv[None, :]
    cos = jnp.cos(ang)[None, :, None, :].astype(t.dtype)
    sin = jnp.sin(ang)[None, :, None, :].astype(t.dtype)
    t1, t2 = t[..., :half], t[..., half:]
    return jnp.concatenate([t1 * cos - t2 * sin, t1 * sin + t2 * cos], axis=-1)


def _retention_dir(q, k, v, log_gamma, strict):
    B, H, S, dk = q.shape
    dv = v.shape[-1]
    n = S // RET_CHUNK
    qc = q.reshape(B, H, n, RET_CHUNK, dk)
    kc = k.reshape(B, H, n, RET_CHUNK, dk)
    vc = v.reshape(B, H, n, RET_CHUNK, dv)
    idx = jnp.arange(RET_CHUNK, dtype=f32)
    diff = idx[:, None] - idx[None, :]
    mask = (diff > 0) if strict else (diff >= 0)
    decay = jnp.where(mask[None], jnp.exp(log_gamma[:, None, None] * jnp.maximum(diff, 0.0)[None]), 0.0)
    scores = jnp.einsum('bhnid,bhnjd->bhnij', qc, kc) * decay[None, :, None]
    o_intra = jnp.einsum('bhnij,bhnjv->bhniv', scores, vc)
    zeta = jnp.exp(log_gamma[:, None] * (RET_CHUNK - 1 - idx)[None])
    kv = jnp.einsum('bhnjd,hj,bhnjv->bhndv', kc, zeta, vc)
    chunk_decay = jnp.exp(log_gamma * RET_CHUNK).astype(kv.dtype)[None, :, None, None]

    def step(R, kv_c):
        return R * chunk_decay + kv_c, R

    _, R_prev = lax.scan(step, jnp.zeros((B, H, dk, dv), kv.dtype), jnp.moveaxis(kv, 2, 0))
    R_prev = jnp.moveaxis(R_prev, 0, 2)
    xi = jnp.exp(log_gamma[:, None] * (idx + 1.0)[None])
    o_cross = jnp.einsum('bhnid,hi,bhndv->bhniv', qc, xi, R_prev)
    return (o_intra + o_cross).reshape(B, H, S, dv)


def _retention_branch(rq, rk, rv, rg, decay_logit):
    B, S, _ = rq.shape
    q = _rotary(rq.reshape(B, S, RET_HEADS, RET_DK))
    k = _rotary(rk.reshape(B, S, RET_HEADS, RET_DK)) * (RET_DK ** -0.5)
    v = rv.reshape(B, S, RET_HEADS, RET_DV)
    q, k, v = [t.transpose(0, 2, 1, 3) for t in (q, k, v)]
    log_gamma = jax.nn.log_sigmoid(decay_logit.astype(f32))
    flip = lambda t: jnp.flip(t, axis=2)
    fwd = _retention_dir(q, k, v, log_gamma[0], False)
    bwd = flip(_retention_dir(flip(q), flip(k), flip(v), log_gamma[1], True))
    o = (fwd + bwd).astype(f32)
    mu = o.mean(-1, keepdims=True)
    var = jnp.square(o - mu).mean(-1, keepdims=True)
    o = (o - mu) * lax.rsqrt(var + LN_EPS)
    o = o.transpose(0, 2, 1, 3).reshape(B, S, RET_V_W).astype(rq.dtype)
    return o * jax.nn.silu(rg)


def _t5_bucket(rel):
    nb = NUM_BUCKETS // 2
    max_exact = nb // 2
    ret = jnp.where(rel > 0, nb, 0)
    n = jnp.abs(rel)
    nf = jnp.maximum(n, 1).astype(f32)
    large = max_exact + (jnp.log(nf / max_exact) / math.log(REL_MAX_DIST / max_exact)
                         * (nb - max_exact)).astype(jnp.int32)
    large = jnp.minimum(large, nb - 1)
    return ret + jnp.where(n < max_exact, n, large)


def _dilated_group(q, k, v, bias_tab, window, dil):
    B, S, H, d = q.shape
    half = window // (2 * dil)
    blk = half
    L = S // dil
    nblk = -(-L // blk)
    Lp = nblk * blk
    pad = Lp - L
    z = B * dil

    def to_res(t):
        return t.reshape(B, L, dil, H, d).transpose(0, 2, 1, 3, 4).reshape(z, L, H, d)

    qb = jnp.pad(to_res(q) * (d ** -0.5), ((0, 0), (0, pad), (0, 0), (0, 0))).reshape(z, nblk, blk, H, d)

    def windows(t):
        tp = jnp.pad(to_res(t), ((0, 0), (blk, pad + blk), (0, 0), (0, 0))).reshape(z, nblk + 2, blk, H, d)
        return jnp.concatenate([tp[:, :-2], tp[:, 1:-1], tp[:, 2:]], axis=2)

    kw, vw = windows(k), windows(v)
    i = jnp.arange(blk)[:, None]
    j = jnp.arange(3 * blk)[None, :]
    off = j - blk - i
    kpos = jnp.arange(nblk)[:, None, None] * blk + (j - blk)[None]
    valid = (jnp.abs(off) <= half)[None] & (kpos >= 0) & (kpos < L)
    bias = bias_tab[_t5_bucket(off * dil)].astype(f32).transpose(2, 0, 1)
    logits = jnp.einsum('znihd,znjhd->znhij', qb, kw).astype(f32) + bias
    logits = jnp.where(valid[None, :, None], logits, NEG_INF)
    m = logits.max(-1, keepdims=True)
    p = jnp.exp(logits - m)
    den = p.sum(-1)
    o = jnp.einsum('znhij,znjhd->znihd', p, vw.astype(f32)) / jnp.swapaxes(den, 2, 3)[..., None]
    lse = jnp.swapaxes(m[..., 0] + jnp.log(den), 2, 3)
    o = o.reshape(z, Lp, H, d)[:, :L].reshape(B, dil, L, H, d).transpose(0, 2, 1, 3, 4).reshape(B, S, H, d)
    lse = lse.reshape(z, Lp, H)[:, :L].reshape(B, dil, L, H).transpose(0, 2, 1, 3).reshape(B, S, H)
    return o, lse


def _dilated_branch(aq, ak, av, rel_bias):
    B, S, _ = aq.shape
    shp = (B, S, len(ATT_GROUPS), ATT_HEADS, ATT_DH)
    q, k, v = aq.reshape(shp), ak.reshape(shp), av.reshape(shp)
    outs, lses = [], []
    for gi, (w, r) in enumerate(ATT_GROUPS):
        o, l = _dilated_group(q[:, :, gi], k[:, :, gi], v[:, :, gi],
                              rel_bias[:, gi * ATT_HEADS:(gi + 1) * ATT_HEADS], w, r)
        outs.append(o)
        lses.append(l)
    wts = jax.nn.softmax(jnp.stack(lses, 0), axis=0)
    o = jnp.sum(wts[..., None] * jnp.stack(outs, 0), axis=0)
    return o.reshape(B, S, ATT_OUT_W).astype(aq.dtype)


def _peer(x, wq, keys, u_tab, v_tab):
    B, S, D = x.shape
    T = B * S
    xt = x.reshape(T, D)
    q = (xt @ wq).reshape(T, PEER_HEADS, 2, PEER_QHALF)
    s = jnp.einsum('thcd,hckd->thck', q, keys).astype(f32)
    s_top, i_top = lax.top_k(s, PEER_TOPK)
    cand = s_top[:, :, 0, :, None] + s_top[:, :, 1, None, :]
    cand_idx = i_top[:, :, 0, :, None] * PEER_NKEYS + i_top[:, :, 1, None, :]
    best, pos = lax.top_k(cand.reshape(T, PEER_HEADS, PEER_TOPK * PEER_TOPK), PEER_TOPK)
    eidx = jnp.take_along_axis(cand_idx.reshape(T, PEER_HEADS, PEER_TOPK * PEER_TOPK), pos, axis=-1)
    g = jax.nn.softmax(best, axis=-1).astype(x.dtype)
    nb = T // PEER_BLOCK

    def blk(args):
        xb, eb, gb = args
        u = u_tab[eb]
        h = jax.nn.gelu(jnp.einsum('td,thkd->thk', xb, u), approximate=False)
        vv = v_tab[eb]
        return jnp.einsum('thk,thkd->td', gb * h, vv)

    out = lax.map(blk, (xt.reshape(nb, PEER_BLOCK, D),
                        eidx.reshape(nb, PEER_BLOCK, PEER_HEADS, PEER_TOPK),
                        g.reshape(nb, PEER_BLOCK, PEER_HEADS, PEER_TOPK)))
    return out.reshape(B, S, D).astype(x.dtype)


def _layer(x, p, w_in, ret_decay_logit, w_ret_o, w_att_o, w_out, rel_bias, ln1_g, ln1_b,
           peer_wq, peer_keys, peer_u, peer_v, w_pe, w_pg, ln2_g, ln2_b):
    proj = x @ w_in
    rq, rk, rv, rg, aq, ak, av, ga, gb = jnp.split(proj, SPLIT_IDX, axis=-1)
    ret = _retention_branch(rq, rk, rv, rg, ret_decay_logit)
    att = _dilated_branch(aq, ak, av, rel_bias)
    merged = jax.nn.sigmoid(ga) * (ret @ w_ret_o) + jax.nn.sigmoid(gb) * (att @ w_att_o)
    h = _layernorm(DN_ALPHA * x + merged @ w_out, ln1_g, ln1_b)
    ff = _peer(h, peer_wq, peer_keys, peer_u, peer_v)
    pe = (p @ w_pe) * jax.nn.sigmoid(h @ w_pg)
    return _layernorm(DN_ALPHA * h + ff + pe, ln2_g, ln2_b)


def setup_inputs(seed: int = 0) -> dict:
    key = jax.random.key(seed)
    ks = jax.random.split(key, 24)
    nrm = lambda k, shape, s: jax.random.normal(k, shape, f32) * s
    x_prompt = nrm(ks[0], (BATCH, SEQ, D_MODEL), 1.0)
    x_sample = nrm(ks[1], (DEC_BATCH, DEC_SEQ, D_MODEL), 1.0)
    p_prompt = nrm(ks[2], (DEPTH, BATCH, SEQ, P_DIM), 1.0)
    p_sample = nrm(ks[3], (DEPTH, DEC_BATCH, DEC_SEQ, P_DIM), 1.0)
    col_scale = jnp.concatenate([jnp.full((n,), s, f32) for n, s in
                                 zip(SPLITS, (1.0, 1.0, DN_BETA, 1.0, 1.0, 1.0, DN_BETA, 1.0, 1.0))])
    w_in = nrm(ks[4], (DEPTH, D_MODEL, IN_W), D_MODEL ** -0.5) * col_scale
    base = jnp.log(2.0 ** (5.0 + jnp.arange(RET_HEADS, dtype=f32)) - 1.0)
    ret_decay_logit = base[None, None, :] + nrm(ks[5], (DEPTH, 2, RET_HEADS), 0.05)
    w_ret_o = nrm(ks[6], (DEPTH, RET_V_W, D_MODEL), DN_BETA * RET_V_W ** -0.5)
    w_att_o = nrm(ks[7], (DEPTH, ATT_OUT_W, D_MODEL), DN_BETA * ATT_OUT_W ** -0.5)
    w_out = nrm(ks[8], (DEPTH, D_MODEL, D_MODEL), DN_BETA * D_MODEL ** -0.5)
    rel_bias = nrm(ks[9], (NUM_BUCKETS, N_ATT_HEADS), 0.2)
    ln1_g = 1.0 + nrm(ks[10], (DEPTH, D_MODEL), 0.02)
    ln1_b = nrm(ks[11], (DEPTH, D_MODEL), 0.02)
    peer_wq = nrm(ks[12], (DEPTH, D_MODEL, PEER_HEADS * PEER_QDIM), D_MODEL ** -0.5)
    peer_keys = nrm(ks[13], (DEPTH, PEER_HEADS, 2, PEER_NKEYS, PEER_QHALF), PEER_QHALF ** -0.5)
    peer_u = nrm(ks[14], (DEPTH, PEER_N, D_MODEL), D_MODEL ** -0.5)
    peer_v = nrm(ks[15], (DEPTH, PEER_N, D_MODEL), DN_BETA * PEER_HEADS ** -0.5)
    w_pe = nrm(ks[16], (DEPTH, P_DIM, D_MODEL), DN_BETA * P_DIM ** -0.5)
    w_pg = nrm(ks[17], (DEPTH, D_MODEL, D_MODEL), D_MODEL ** -0.5)
    ln2_g = 1.0 + nrm(ks[18], (DEPTH, D_MODEL), 0.02)
    ln2_b = nrm(ks[19], (DEPTH, D_MODEL), 0.02)
    return {"x_prompt": x_prompt, "x_sample": x_sample, "p_prompt": p_prompt, "p_sample": p_sample,
            "w_in": w_in, "ret_decay_logit": ret_decay_logit, "w_ret_o": w_ret_o, "w_att_o": w_att_o,
            "w_out": w_out, "rel_bias": rel_bias, "ln1_g": ln1_g, "ln1_b": ln1_b,
            "peer_wq": peer_wq, "peer_keys": peer_keys, "peer_u": peer_u, "peer_v": peer_v,
            "w_pe": w_pe, "w_pg": w_pg, "ln2_g": ln2_g, "ln2_b": ln2_b}


def reference(x_prompt, x_sample, p_prompt, p_sample, w_in, ret_decay_logit, w_ret_o, w_att_o,
              w_out, rel_bias, ln1_g, ln1_b, peer_wq, peer_keys, peer_u, peer_v, w_pe, w_pg,
              ln2_g, ln2_b):
    y_prompt = x_prompt
    y_sample = x_sample
    for i in range(DEPTH):
        y_prompt = _layer(y_prompt, p_prompt[i], w_in[i], ret_decay_logit[i], w_ret_o[i], w_att_o[i],
                          w_out[i], rel_bias, ln1_g[i], ln1_b[i], peer_wq[i], peer_keys[i], peer_u[i],
                          peer_v[i], w_pe[i], w_pg[i], ln2_g[i], ln2_b[i])
        y_sample = _layer(y_sample, p_sample[i], w_in[i], ret_decay_logit[i], w_ret_o[i], w_att_o[i],
                          w_out[i], rel_bias, ln1_g[i], ln1_b[i], peer_wq[i], peer_keys[i], peer_u[i],
                          peer_v[i], w_pe[i], w_pg[i], ln2_g[i], ln2_b[i])
    return (y_prompt, y_sample)
```

```python
import contextlib
import math
import numpy as np
import concourse.bass as bass
import concourse.mybir as mybir
from concourse.bass_utils import run_bass_kernel_spmd

F32 = mybir.dt.float32
BF16 = mybir.dt.bfloat16
AF = mybir.ActivationFunctionType
ALU = mybir.AluOpType

D = 2048
KC = 16
IN_W = 19456
HALO = 1024
GROUPS = ((128, 1), (512, 4), (2048, 16))
DN_ALPHA = 2.0 ** 0.25
LN_EPS = 1e-5
NKEY = 128


class DSem:
    def __init__(self, sem):
        self.sem = sem
        self.cnt = 0


class Buf:
    def __init__(self, ctx, t, name):
        self.ctx, self.t, self.name = ctx, t, name
        self.w, self.r, self.rp = {}, {}, {}
        self.ds = None

    def __getitem__(self, idx):
        return self.t[idx]

    def dsem(self):
        if self.ds is None:
            self.ds = self.ctx.get_dsem()
        return self.ds


class Ctx:
    def __init__(self, nc, es):
        self.nc, self.es = nc, es
        self.eng = {"pe": nc.tensor, "act": nc.scalar, "dve": nc.vector, "pool": nc.gpsimd, "sp": nc.sync}
        self.sem, self.cnt, self.known = {}, {}, {}
        self.semobj = {}
        for k in self.eng:
            self.sem[k] = es.enter_context(nc.semaphore("e_" + k))
            self.semobj[id(self.sem[k])] = self.sem[k]
            self.cnt[k] = 0
            self.known[k] = {}
        self.free_ds, self.all_ds = [], []
        self.phase_bufs = []
        self.pes = None
        self.uid = 0

    def get_dsem(self):
        if self.free_ds:
            return self.free_ds.pop()
        d = DSem(self.es.enter_context(self.nc.semaphore("d%d" % len(self.all_ds))))
        self.semobj[id(d.sem)] = d.sem
        self.all_ds.append(d)
        return d

    def begin_phase(self):
        self.pes = contextlib.ExitStack()
        self.phase_bufs = []

    def end_phase(self):
        self.barrier()
        for b in self.phase_bufs:
            if b.ds is not None:
                self.free_ds.append(b.ds)
                b.ds = None
        self.pes.close()
        self.pes = None

    def barrier(self):
        toks = {}
        for k in self.eng:
            if self.cnt[k]:
                toks[id(self.sem[k])] = self.cnt[k]
        for d in self.all_ds:
            if d.cnt:
                toks[id(d.sem)] = d.cnt
        for k in self.eng:
            self._wait(k, toks)

    def sb(self, name, shape, dt):
        self.uid += 1
        t = self.pes.enter_context(self.nc.sbuf_tensor("%s_%d" % (name, self.uid), shape, dt))
        b = Buf(self, t, name)
        self.phase_bufs.append(b)
        return b

    def ps(self, name, shape, dt=F32):
        self.uid += 1
        t = self.pes.enter_context(self.nc.psum_tensor("%s_%d" % (name, self.uid), shape, dt))
        b = Buf(self, t, name)
        self.phase_bufs.append(b)
        return b

    def dram(self, name, shape, dt, kind="Internal"):
        t = self.nc.dram_tensor(name, shape, dt, kind=kind).ap()
        return Buf(self, t, name)

    def _merge(self, toks, d):
        for k, v in d.items():
            if toks.get(k, 0) < v:
                toks[k] = v

    def _wait(self, e, toks, skip_self=False):
        kn = self.known[e]
        me = id(self.sem[e])
        for k, v in toks.items():
            if k == me and (skip_self or v > self.cnt[e]):
                continue
            if kn.get(k, 0) >= v:
                continue
            self.eng[e].wait_ge(self.semobj[k], v)
            kn[k] = v

    def _deps(self, reads, writes, par=False):
        toks = {}
        for b in reads:
            self._merge(toks, b.w)
        for b in writes:
            if par:
                self._merge(toks, b.rp)
            else:
                self._merge(toks, b.w)
            self._merge(toks, b.r)
        return toks

    def _commit(self, key, v, reads, writes, acc):
        for b in reads:
            if b.r.get(key, 0) < v:
                b.r[key] = v
        for b in writes:
            if acc:
                if b.w.get(key, 0) < v:
                    b.w[key] = v
            else:
                rp = dict(b.w)
                self._merge(rp, b.r)
                b.rp = rp
                b.w = {key: v}
                b.r = {}

    def op(self, e, fn, reads=(), writes=(), acc=False, par=False):
        acc = acc or par
        self._wait(e, self._deps(reads, writes, par), skip_self=(e == "pe"))
        inst = fn(self.eng[e])
        self.cnt[e] += 1
        inst.then_inc(self.sem[e], 1)
        self._commit(id(self.sem[e]), self.cnt[e], reads, writes, acc)
        return inst

    def dma(self, q, out_ap, in_ap, reads=(), writes=(), sbuf=None, acc=False, par=False):
        acc = acc or par
        self._wait(q, self._deps(reads, writes, par))
        inst = self.eng[q].dma_start(out=out_ap, in_=in_ap)
        d = sbuf.dsem()
        d.cnt += 16
        inst.then_inc(d.sem, 16)
        self._commit(id(d.sem), d.cnt, reads, writes, acc)
        return inst


def t5_bucket_np(rel):
    nb, max_exact = 16, 8
    rel = np.asarray(rel, np.int64)
    ret = np.where(rel > 0, nb, 0)
    n = np.abs(rel)
    nf = np.maximum(n, 1).astype(np.float32)
    large = max_exact + (np.log(nf / np.float32(max_exact)) / np.float32(math.log(1024 / max_exact))
                         * np.float32(nb - max_exact)).astype(np.int32)
    large = np.minimum(large, nb - 1)
    return ret + np.where(n < max_exact, n, large)


def att_mask_tables():
    out = []
    j = np.arange(128)[:, None]
    i = np.arange(128)[None, :]
    for (_w, r) in GROUPS:
        keys, masks = [], []
        for kt in range(2):
            off = j - 64 - i if kt == 0 else j + 64 - i
            band = np.abs(off) <= 64
            bk = t5_bucket_np(off * r)
            for b in range(32):
                m = band & (bk == b)
                if m.any():
                    keys.append((kt, b))
                    masks.append(m.astype(np.float32))
        out.append((keys, np.ascontiguousarray(np.stack(masks, 1))))
    return out


ATT_TABLES = att_mask_tables()


def ret_consts(T):
    NT = T // 128
    j = np.arange(128, dtype=np.float32)[:, None]
    i = np.arange(128, dtype=np.float32)[None, :]
    rc = np.zeros((128, 6 * 128 + 2 + 2 * NT), np.float32)
    rc[:, 0:128] = np.maximum(i - j, 0)
    rc[:, 128:256] = (i >= j)
    rc[:, 256:384] = np.maximum(j - i, 0)
    rc[:, 384:512] = (j > i)
    rc[:, 512:640] = i + 1
    rc[:, 640:768] = 128 - i
    rc[:, 768] = 127 - j[:, 0]
    rc[:, 769] = j[:, 0]
    n = np.arange(NT, dtype=np.float32)[None, :]
    rc[:, 770:770 + NT] = T - 1 - (128 * n + j)
    rc[:, 770 + NT:770 + 2 * NT] = 128 * n + j
    return rc


def misc_consts():
    mc = np.zeros((128, 4 * 128), np.float32)
    mc[:, 0:128] = np.eye(128)
    m = np.arange(128)
    mc[(m + 64) % 128, 128 + m] = 1.0
    mc[:, 256:384] = np.arange(128)[None, :]
    mc[:, 384:400] = np.arange(16)[None, :]
    mc[:, 400:416] = 16.0 * (np.arange(16)[None, :] + 1)
    return mc


def rot_table(pos):
    half = 64
    inv = (1.0 / (10000.0 ** np.linspace(0.0, 1.0, half, dtype=np.float32))).astype(np.float32)
    ang = pos.astype(np.float32)[None, :] * inv[:, None]
    cos = np.cos(ang).astype(np.float32)
    sin = np.sin(ang).astype(np.float32)
    cs = np.empty((128, 2, pos.shape[0]), np.float32)
    cs[:64, 0], cs[64:, 0] = cos, cos
    cs[:64, 1], cs[64:, 1] = -sin, sin
    return cs


def build(T, upto=99, debug=False):
    NT = T // 128
    EXT = T + 2 * HALO
    NG = [T + 128 * r for (_w, r) in GROUPS]
    nc = bass.Bass("TRN2", target_bir_lowering=False)
    es = contextlib.ExitStack()
    with es:
        c = Ctx(nc, es)
        dbg_names = set(debug) if debug else set()

        def inp(name, shape):
            return c.dram(name, shape, F32, kind="ExternalInput")

        xT_ext = inp("xT_ext", [D, EXT])
        xT_nbr = inp("xT_nbr", [D, T])
        x_tm = inp("x_tm", [T, D])
        pT = inp("pT", [256, T])
        flags = inp("flags", [128, 4])
        cs_own = inp("cs_own", [128, 2, T])
        cs_nbr = inp("cs_nbr", [128, 2, T])
        rcst = inp("rcst", [128, 770 + 2 * NT])
        mcst = inp("mcst", [128, 512])
        am_in = [inp("am%d" % g, [128, len(ATT_TABLES[g][0]), 128]) for g in range(3)]
        w_in = inp("w_in", [D, IN_W])
        decay = inp("ret_decay_logit", [2, 8])
        w_ret_o = inp("w_ret_o", [D, D])
        w_att_o = inp("w_att_o", [1024, D])
        w_out = inp("w_out", [D, D])
        rel_bias = inp("rel_bias", [32, 24])
        ln1_g, ln1_b = inp("ln1_g", [1, D]), inp("ln1_b", [1, D])
        ln2_g, ln2_b = inp("ln2_g", [1, D]), inp("ln2_b", [1, D])
        peer_wq = inp("peer_wq", [D, D])
        keysT = inp("keysT", [16, 128, 128])
        uT = inp("uT", [128, KC, 128, 128])
        vj = inp("vj", [128, 128, D])
        w_pe = inp("w_pe", [256, D])
        w_pg = inp("w_pg", [D, D])
        y = c.dram("y", [T, D], F32, kind="ExternalOutput")

        def scr(name, shape, dt=BF16):
            return c.dram(name, shape, dt, kind=("ExternalOutput" if name in dbg_names else "Internal"))

        xb_nat = scr("xb_nat", [KC, 128, EXT])
        xb_r4 = scr("xb_r4", [KC, 128, NG[1]])
        xb_r16 = scr("xb_r16", [KC, 128, NG[2]])
        xb_nbr = scr("xb_nbr", [KC, 128, T])
        pb = scr("pb", [2, 128, T])
        rqT, rkT, rkT_n = scr("rqT", [1024, T]), scr("rkT", [1024, T]), scr("rkT_n", [1024, T])
        rv, rv_n, srg = scr("rv", [T, D]), scr("rv_n", [T, D]), scr("srg", [T, D])
        aqT = [scr("aqT%d" % g, [1024, NG[g]]) for g in range(3)]
        akT = [scr("akT%d" % g, [1024, NG[g]]) for g in range(3)]
        av = [scr("av%d" % g, [NG[g], 1024]) for g in range(3)]
        sgaT, sgbT = scr("sgaT", [D, T]), scr("sgbT", [D, T])
        retT = scr("retT", [D, T])
        att_u = [scr("att_u%d" % g, [T, 8, 132], F32) for g in range(3)]
        attT = scr("attT", [1024, T])
        mergedT = scr("mergedT", [D, T])
        h_tm = scr("h_tm", [T, D], F32)
        hT = scr("hT", [KC, 128, T])
        IT, JT, gT = scr("IT", [128, T], F32), scr("JT", [128, T], F32), scr("gT", [128, T], F32)
        base = scr("base", [T, D], F32)
        uTb = scr("uTb", [128, 128, KC, 128])
        vb = scr("vb", [128, 128, D])

        def phase1():
            c.begin_phase()
            xin = [c.sb("xin%d" % k, [128, EXT], F32) for k in range(2)]
            onat = [c.sb("onat%d" % k, [128, EXT], BF16) for k in range(2)]
            o4 = [c.sb("o4%d" % k, [128, NG[1]], BF16) for k in range(2)]
            o16 = [c.sb("o16%d" % k, [128, NG[2]], BF16) for k in range(2)]
            for dc in range(KC):
                k = dc % 2
                c.dma("sp", xin[k][:], xT_ext[dc * 128:(dc + 1) * 128, :], [xT_ext], [xin[k]], sbuf=xin[k])
                c.op("act", lambda e: e.copy(out=onat[k][:], in_=xin[k][:]), [xin[k]], [onat[k]])
                c.op("dve", lambda e: e.tensor_copy(
                    out=o4[k][:].rearrange("p (z q) -> p z q", z=4),
                    in_=xin[k][:, HALO - 256:HALO - 256 + NG[1]].rearrange("p (q z) -> p z q", z=4)),
                    [xin[k]], [o4[k]])
                c.op("pool", lambda e: e.tensor_copy(
                    out=o16[k][:].rearrange("p (z q) -> p z q", z=16),
                    in_=xin[k][:].rearrange("p (q z) -> p z q", z=16)), [xin[k]], [o16[k]])
                c.dma("pool", xb_nat[dc], onat[k][:], [onat[k]], [xb_nat], sbuf=onat[k], acc=True)
                c.dma("pool", xb_r4[dc], o4[k][:], [o4[k]], [xb_r4], sbuf=o4[k], acc=True)
                c.dma("pool", xb_r16[dc], o16[k][:], [o16[k]], [xb_r16], sbuf=o16[k], acc=True)
            for dc in range(KC):
                k = dc % 2
                c.dma("sp", xin[k][:, 0:T], xT_nbr[dc * 128:(dc + 1) * 128, :], [xT_nbr], [xin[k]], sbuf=xin[k])
                c.op("act", lambda e: e.copy(out=onat[k][:, 0:T], in_=xin[k][:, 0:T]), [xin[k]], [onat[k]])
                c.dma("pool", xb_nbr[dc], onat[k][:, 0:T], [onat[k]], [xb_nbr], sbuf=onat[k], acc=True)
            for dc in range(2):
                k = dc % 2
                c.dma("sp", xin[k][:, 0:T], pT[dc * 128:(dc + 1) * 128, :], [pT], [xin[k]], sbuf=xin[k])
                c.op("act", lambda e: e.copy(out=onat[k][:, 0:T], in_=xin[k][:, 0:T]), [xin[k]], [onat[k]])
                c.dma("pool", pb[dc], onat[k][:, 0:T], [onat[k]], [pb], sbuf=onat[k], acc=True)
            c.end_phase()

        def gemm_phase(jobs):
            c.begin_phase()
            SBK, CB = 2048, 512
            act = c.sb("g_act", [128, KC, SBK], BF16)
            wst = [c.sb("g_wst%d" % k, [128, KC, CB], F32) for k in range(2)]
            wbf = [c.sb("g_wbf%d" % k, [128, KC, CB], BF16) for k in range(2)]
            pss = [c.ps("g_ps%d" % k, [128, 512], F32) for k in range(4)]
            ost = [c.sb("g_ost%d" % k, [128, 512], BF16) for k in range(4)]
            pi = [0]
            wi = [0]

            def load_w(task):
                (wd, wc0, mode, obuf, oc0, func) = task
                k = wi[0] % 2
                wi[0] += 1
                c.dma("sp", wst[k][:], wd[:, wc0:wc0 + CB].rearrange("(kc p) n -> p kc n", p=128), [wd], [wst[k]], sbuf=wst[k])
                c.op("dve", lambda e: e.tensor_copy(out=wbf[k][:, 0:8, :], in_=wst[k][:, 0:8, :]), [wst[k]], [wbf[k]])
                c.op("pool", lambda e: e.tensor_copy(out=wbf[k][:, 8:16, :], in_=wst[k][:, 8:16, :]), [wst[k]], [wbf[k]], par=True)
                return k

            def compute(task, k, s0, sn):
                (wd, wc0, mode, obuf, oc0, func) = task
                if mode == "fm":
                    for cc in range(CB // 128):
                        for t0 in range(0, sn, 512):
                            tn = min(512, sn - t0)
                            p = pi[0] % 4
                            pi[0] += 1
                            for kc in range(KC):
                                c.op("pe", lambda e: e.matmul(pss[p][:, 0:tn], lhsT=wbf[k][:, kc, cc * 128:(cc + 1) * 128],
                                                            rhs=act[:, kc, t0:t0 + tn], start=(kc == 0), stop=(kc == KC - 1)),
                                     [wbf[k], act], [pss[p]])
                            c.op("act", lambda e: e.activation(out=ost[p][:, 0:tn], in_=pss[p][:, 0:tn], func=func), [pss[p]], [ost[p]])
                            r0 = oc0 + cc * 128
                            c.dma("sp", obuf[r0:r0 + 128, s0 + t0:s0 + t0 + tn], ost[p][:, 0:tn], [ost[p]], [obuf], sbuf=ost[p], par=True)
                else:
                    for t0 in range(0, sn, 128):
                        p = pi[0] % 4
                        pi[0] += 1
                        for kc in range(KC):
                            c.op("pe", lambda e: e.matmul(pss[p][:, 0:CB], lhsT=act[:, kc, t0:t0 + 128],
                                                        rhs=wbf[k][:, kc, :], start=(kc == 0), stop=(kc == KC - 1)),
                                 [wbf[k], act], [pss[p]])
                        c.op("act", lambda e: e.activation(out=ost[p][:, 0:CB], in_=pss[p][:, 0:CB], func=func), [pss[p]], [ost[p]])
                        c.dma("sp", obuf[s0 + t0:s0 + t0 + 128, oc0:oc0 + CB], ost[p][:, 0:CB], [ost[p]], [obuf], sbuf=ost[p], par=True)

            for (abuf, tok0, ntok, cols) in jobs:
                tasks = []
                for (wd, col0, ncols, mode, obuf, oc0, func) in cols:
                    for cb0 in range(0, ncols, CB):
                        tasks.append((wd, col0 + cb0, mode, obuf, oc0 + cb0, func))
                for s0 in range(0, ntok, SBK):
                    sn = min(SBK, ntok - s0)
                    for kc in range(KC):
                        c.dma("sp", act[:, kc, 0:sn], abuf[kc][:, tok0 + s0:tok0 + s0 + sn], [abuf], [act],
                              sbuf=act, par=(kc > 0))
                    knext = load_w(tasks[0])
                    for i, task in enumerate(tasks):
                        k = knext
                        if i + 1 < len(tasks):
                            knext = load_w(tasks[i + 1])
                        compute(task, k, s0, sn)
            c.end_phase()

        def phaseA():
            CP = AF.Copy
            jobs = [
                (xb_nat, HALO, T, [(w_in, 0, 1024, "fm", rqT, 0, CP), (w_in, 1024, 1024, "fm", rkT, 0, CP),
                                   (w_in, 2048, 2048, "tm", rv, 0, CP), (w_in, 4096, 2048, "tm", srg, 0, AF.Silu),
                                   (w_in, 15360, 2048, "fm", sgaT, 0, AF.Sigmoid), (w_in, 17408, 2048, "fm", sgbT, 0, AF.Sigmoid)]),
                (xb_nbr, 0, T, [(w_in, 1024, 1024, "fm", rkT_n, 0, CP), (w_in, 2048, 2048, "tm", rv_n, 0, CP)]),
            ]
            gx = [(xb_nat, HALO - 64), (xb_r4, 0), (xb_r16, 0)]
            for g in range(3):
                jobs.append((gx[g][0], gx[g][1], NG[g],
                             [(w_in, 6144 + g * 1024, 1024, "fm", aqT[g], 0, CP), (w_in, 9216 + g * 1024, 1024, "fm", akT[g], 0, CP),
                              (w_in, 12288 + g * 1024, 1024, "tm", av[g], 0, CP)]))
            gemm_phase(jobs)

        def phase2():
            c.begin_phase()
            rc = c.sb("rc", [128, 770 + 2 * NT], F32)
            mc = c.sb("mc", [128, 512], F32)
            fl = c.sb("fl", [128, 4], F32)
            lg = c.sb("lg", [128, 16], F32)
            c.dma("sp", rc[:], rcst[:], [rcst], [rc], sbuf=rc)
            c.dma("sp", mc[:], mcst[:], [mcst], [mc], sbuf=mc)
            c.dma("sp", fl[:], flags[:], [flags], [fl], sbuf=fl)
            c.dma("sp", lg[:], decay[:].rearrange("a b -> (a b)").partition_broadcast(128), [decay], [lg], sbuf=lg)
            c.op("act", lambda e: e.activation(out=lg[:], in_=lg[:], func=AF.Exp, scale=-1.0), [lg], [lg])
            c.op("dve", lambda e: e.tensor_scalar(out=lg[:], in0=lg[:], scalar1=1.0, scalar2=None, op0=ALU.add), [lg], [lg])
            c.op("act", lambda e: e.activation(out=lg[:], in_=lg[:], func=AF.Ln), [lg], [lg])
            c.op("dve", lambda e: e.tensor_scalar(out=lg[:], in0=lg[:], scalar1=-1.0, scalar2=None, op0=ALU.mult), [lg], [lg])
            identb = c.sb("identb", [128, 128], BF16)
            permb = c.sb("permb", [128, 128], BF16)
            c.op("dve", lambda e: e.tensor_copy(out=identb[:], in_=mc[:, 0:128]), [mc], [identb])
            c.op("dve", lambda e: e.tensor_copy(out=permb[:], in_=mc[:, 128:256]), [mc], [permb])
            qR, kR, kRn = c.sb("qR", [128, T], BF16), c.sb("kR", [128, T], BF16), c.sb("kRn", [128, T], BF16)
            raw = [c.sb("raw%d" % k, [128, 512], BF16) for k in range(2)]
            cst = [c.sb("cst%d" % k, [128, 2, 512], F32) for k in range(2)]
            tmp1 = [c.sb("tmp1%d" % k, [128, 512], F32) for k in range(2)]
            tmp2 = [c.sb("tmp2%d" % k, [128, 512], F32) for k in range(2)]
            vt = c.sb("vt", [128, NT, 256], BF16)
            vtn = c.sb("vtn", [128, NT, 256], BF16)
            gt = c.sb("gt", [128, NT, 256], BF16)
            Rfs = c.sb("Rfs", [128, NT, 256], BF16)
            Rbs = c.sb("Rbs", [128, NT, 256], BF16)
            kbs = c.sb("kbs", [128, NT, 128], BF16)
            retTs = c.sb("retTs", [128, 2, T], BF16)
            DT = c.sb("DT", [128, 128], F32)
            dtmp = c.sb("dtmp", [128, 128], F32)
            xif, xib = c.sb("xif", [128, 128], F32), c.sb("xib", [128, 128], F32)
            hv = c.sb("hv", [128, 8 + 2 * NT], F32)
            Rf, Rb = c.sb("Rf", [128, 256], F32), c.sb("Rb", [128, 256], F32)
            kf = [c.sb("kf%d" % k, [128, 128], BF16) for k in range(2)]
            kb2 = [c.sb("kb2%d" % k, [128, 128], BF16) for k in range(2)]
            STs = [c.sb("STs%d" % k, [128, 128], BF16) for k in range(2)]
            qfs = [c.sb("qfs%d" % k, [128, 128], BF16) for k in range(2)]
            qbs = [c.sb("qbs%d" % k, [128, 128], BF16) for k in range(2)]
            st6 = [c.sb("st6%d" % k, [128, 8], F32) for k in range(2)]
            mv = [c.sb("mv%d" % k, [128, 4], F32) for k in range(2)]
            on = [c.sb("on%d" % k, [128, 256], F32) for k in range(2)]
            rt = [c.sb("rt%d" % k, [128, 256], BF16) for k in range(2)]
            ps_r = [c.ps("ps_r%d" % k, [128, 512], F32) for k in range(2)]
            ps_t = [c.ps("ps_t%d" % k, [128, 256], BF16) for k in range(2)]
            ps_k = [c.ps("ps_k%d" % k, [128, 256], F32) for k in range(2)]
            ps_s = c.ps("ps_s", [128, 128], F32)
            ps_o = c.ps("ps_o", [128, 256], F32)
            SC = 128.0 ** -0.5
            blk = [0]

            def rotary(src, cs, dst, h):
                for t0 in range(0, T, 512):
                    k = blk[0] % 2
                    blk[0] += 1
                    c.dma("sp", raw[k][:], src[h * 128:(h + 1) * 128, t0:t0 + 512], [src], [raw[k]], sbuf=raw[k])
                    c.dma("sp", cst[k][:], cs[:, :, t0:t0 + 512], [cs], [cst[k]], sbuf=cst[k])
                    c.op("pe", lambda e: e.matmul(ps_r[k][:], lhsT=permb[:], rhs=raw[k][:], start=True, stop=True), [permb, raw[k]], [ps_r[k]])
                    c.op("pool", lambda e: e.tensor_tensor(out=tmp1[k][:], in0=raw[k][:], in1=cst[k][:, 0, :], op=ALU.mult), [raw[k], cst[k]], [tmp1[k]])
                    c.op("dve", lambda e: e.tensor_tensor(out=tmp2[k][:], in0=ps_r[k][:], in1=cst[k][:, 1, :], op=ALU.mult), [ps_r[k], cst[k]], [tmp2[k]])
                    c.op("dve", lambda e: e.tensor_tensor(out=dst[:, t0:t0 + 512], in0=tmp1[k][:], in1=tmp2[k][:], op=ALU.add), [tmp1[k], tmp2[k]], [dst], acc=True)

            for h in range(8):
                lf, lb = lg[:, h:h + 1], lg[:, 8 + h:9 + h]
                c.op("act", lambda e: e.activation(out=DT[:], in_=rc[:, 0:128], func=AF.Exp, scale=lf), [rc, lg], [DT])
                c.op("dve", lambda e: e.tensor_tensor(out=DT[:], in0=DT[:], in1=rc[:, 128:256], op=ALU.mult), [DT, rc], [DT])
                c.op("act", lambda e: e.activation(out=dtmp[:], in_=rc[:, 256:384], func=AF.Exp, scale=lb), [rc, lg], [dtmp])
                c.op("dve", lambda e: e.tensor_tensor(out=dtmp[:], in0=dtmp[:], in1=rc[:, 384:512], op=ALU.mult), [dtmp, rc], [dtmp])
                c.op("dve", lambda e: e.scalar_tensor_tensor(out=DT[:], in0=DT[:], scalar=1.0, in1=dtmp[:], op0=ALU.mult, op1=ALU.add), [DT, dtmp], [DT])
                c.op("dve", lambda e: e.tensor_scalar(out=DT[:], in0=DT[:], scalar1=SC, scalar2=None, op0=ALU.mult), [DT], [DT])
                c.op("act", lambda e: e.activation(out=xif[:], in_=rc[:, 512:640], func=AF.Exp, scale=lf), [rc, lg], [xif])
                c.op("act", lambda e: e.activation(out=xib[:], in_=rc[:, 640:768], func=AF.Exp, scale=lb), [rc, lg], [xib])
                c.op("act", lambda e: e.activation(out=hv[:, 0:1], in_=rc[:, 768:769], func=AF.Exp, scale=lf), [rc, lg], [hv])
                c.op("act", lambda e: e.activation(out=hv[:, 1:2], in_=rc[:, 769:770], func=AF.Exp, scale=lb), [rc, lg], [hv])
                c.op("act", lambda e: e.activation(out=hv[:, 8:8 + NT], in_=rc[:, 770:770 + NT], func=AF.Exp, scale=lf), [rc, lg], [hv])
                c.op("act", lambda e: e.activation(out=hv[:, 8 + NT:8 + 2 * NT], in_=rc[:, 770 + NT:770 + 2 * NT], func=AF.Exp, scale=lb), [rc, lg], [hv])
                c.op("dve", lambda e: e.tensor_scalar(out=hv[:, 0:2], in0=hv[:, 0:2], scalar1=SC, scalar2=None, op0=ALU.mult), [hv], [hv])
                c.op("dve", lambda e: e.tensor_scalar(out=hv[:, 8:8 + 2 * NT], in0=hv[:, 8:8 + 2 * NT], scalar1=SC, scalar2=None, op0=ALU.mult), [hv], [hv])
                c.op("dve", lambda e: e.tensor_scalar(out=hv[:, 2:3], in0=lg[:, h:h + 1], scalar1=128.0, scalar2=None, op0=ALU.mult), [lg], [hv])
                c.op("dve", lambda e: e.tensor_scalar(out=hv[:, 3:4], in0=lg[:, 8 + h:9 + h], scalar1=128.0, scalar2=None, op0=ALU.mult), [lg], [hv])
                c.op("act", lambda e: e.activation(out=hv[:, 2:4], in_=hv[:, 2:4], func=AF.Exp), [hv], [hv])
                rotary(rqT, cs_own, qR, h)
                rotary(rkT, cs_own, kR, h)
                rotary(rkT_n, cs_nbr, kRn, h)
                c.dma("sp", vt[:], rv[:, h * 256:(h + 1) * 256].rearrange("(n p) c -> p n c", p=128), [rv], [vt], sbuf=vt)
                c.dma("sp", vtn[:], rv_n[:, h * 256:(h + 1) * 256].rearrange("(n p) c -> p n c", p=128), [rv_n], [vtn], sbuf=vtn)
                c.dma("sp", gt[:], srg[:, h * 256:(h + 1) * 256].rearrange("(n p) c -> p n c", p=128), [srg], [gt], sbuf=gt)
                for n in range(NT):
                    k = n % 2
                    c.op("pe", lambda e: e.transpose(out=ps_t[k][:, 0:128], in_=kRn[:, n * 128:(n + 1) * 128], identity=identb[:]), [kRn, identb], [ps_t[k]])
                    c.op("dve", lambda e: e.tensor_scalar(out=kf[k][:], in0=ps_t[k][:, 0:128], scalar1=hv[:, 8 + n:9 + n], scalar2=None, op0=ALU.mult), [ps_t[k], hv], [kf[k]])
                    c.op("act", lambda e: e.activation(out=kb2[k][:], in_=ps_t[k][:, 0:128], func=AF.Copy, scale=hv[:, 8 + NT + n:9 + NT + n]), [ps_t[k], hv], [kb2[k]])
                    c.op("pe", lambda e: e.matmul(ps_k[0][:], lhsT=kf[k][:], rhs=vtn[:, n, :], start=(n == 0), stop=(n == NT - 1)), [kf[k], vtn], [ps_k[0]])
                    c.op("pe", lambda e: e.matmul(ps_k[1][:], lhsT=kb2[k][:], rhs=vtn[:, n, :], start=(n == 0), stop=(n == NT - 1)), [kb2[k], vtn], [ps_k[1]])
                c.op("dve", lambda e: e.tensor_scalar(out=Rf[:], in0=ps_k[0][:], scalar1=fl[:, 0:1], scalar2=None, op0=ALU.mult), [ps_k[0], fl], [Rf])
                c.op("dve", lambda e: e.tensor_scalar(out=Rb[:], in0=ps_k[1][:], scalar1=fl[:, 1:2], scalar2=None, op0=ALU.mult), [ps_k[1], fl], [Rb])
                for n in range(NT):
                    k = n % 2
                    c.op("act", lambda e: e.copy(out=Rfs[:, n, :], in_=Rf[:]), [Rf], [Rfs], acc=True)
                    c.op("pe", lambda e: e.transpose(out=ps_t[k][:, 0:128], in_=kR[:, n * 128:(n + 1) * 128], identity=identb[:]), [kR, identb], [ps_t[k]])
                    c.op("dve", lambda e: e.tensor_scalar(out=kf[k][:], in0=ps_t[k][:, 0:128], scalar1=hv[:, 0:1], scalar2=None, op0=ALU.mult), [ps_t[k], hv], [kf[k]])
                    c.op("act", lambda e: e.activation(out=kbs[:, n, :], in_=ps_t[k][:, 0:128], func=AF.Copy, scale=hv[:, 1:2]), [ps_t[k], hv], [kbs], acc=True)
                    c.op("pe", lambda e: e.matmul(ps_k[k][:], lhsT=kf[k][:], rhs=vt[:, n, :], start=True, stop=True), [kf[k], vt], [ps_k[k]])
                    c.op("dve", lambda e: e.scalar_tensor_tensor(out=Rf[:], in0=Rf[:], scalar=hv[:, 2:3], in1=ps_k[k][:], op0=ALU.mult, op1=ALU.add), [Rf, hv, ps_k[k]], [Rf])
                for n in range(NT - 1, -1, -1):
                    k = n % 2
                    c.op("act", lambda e: e.copy(out=Rbs[:, n, :], in_=Rb[:]), [Rb], [Rbs], acc=True)
                    c.op("pe", lambda e: e.matmul(ps_k[k][:], lhsT=kbs[:, n, :], rhs=vt[:, n, :], start=True, stop=True), [kbs, vt], [ps_k[k]])
                    c.op("dve", lambda e: e.scalar_tensor_tensor(out=Rb[:], in0=Rb[:], scalar=hv[:, 3:4], in1=ps_k[k][:], op0=ALU.mult, op1=ALU.add), [Rb, hv, ps_k[k]], [Rb])
                for n in range(NT):
                    k = n % 2
                    sl = slice(n * 128, (n + 1) * 128)
                    c.op("pe", lambda e: e.matmul(ps_s[:], lhsT=kR[:, sl], rhs=qR[:, sl], start=True, stop=True), [kR, qR], [ps_s])
                    c.op("dve", lambda e: e.tensor_tensor(out=STs[k][:], in0=ps_s[:], in1=DT[:], op=ALU.mult), [ps_s, DT], [STs[k]])
                    c.op("pool", lambda e: e.tensor_tensor(out=qfs[k][:], in0=qR[:, sl], in1=xif[:], op=ALU.mult), [qR, xif], [qfs[k]])
                    c.op("pool", lambda e: e.tensor_tensor(out=qbs[k][:], in0=qR[:, sl], in1=xib[:], op=ALU.mult), [qR, xib], [qbs[k]])
                    c.op("pe", lambda e: e.matmul(ps_o[:], lhsT=STs[k][:], rhs=vt[:, n, :], start=True, stop=False), [STs[k], vt], [ps_o])
                    c.op("pe", lambda e: e.matmul(ps_o[:], lhsT=qfs[k][:], rhs=Rfs[:, n, :], start=False, stop=False), [qfs[k], Rfs], [ps_o])
                    c.op("pe", lambda e: e.matmul(ps_o[:], lhsT=qbs[k][:], rhs=Rbs[:, n, :], start=False, stop=True), [qbs[k], Rbs], [ps_o])
                    c.op("dve", lambda e: e.bn_stats(out=st6[k][:, 0:6], in_=ps_o[:]), [ps_o], [st6[k]])
                    c.op("dve", lambda e: e.bn_aggr(out=mv[k][:, 0:2], in_=st6[k][:, 0:6]), [st6[k]], [mv[k]])
                    c.op("dve", lambda e: e.tensor_scalar(out=mv[k][:, 2:3], in0=mv[k][:, 1:2], scalar1=LN_EPS, scalar2=None, op0=ALU.add), [mv[k]], [mv[k]])
                    c.op("act", lambda e: e.sqrt(out=mv[k][:, 2:3], in_=mv[k][:, 2:3]), [mv[k]], [mv[k]])
                    c.op("dve", lambda e: e.reciprocal(out=mv[k][:, 3:4], in_=mv[k][:, 2:3]), [mv[k]], [mv[k]])
                    c.op("dve", lambda e: e.tensor_scalar(out=on[k][:], in0=ps_o[:], scalar1=mv[k][:, 0:1], scalar2=mv[k][:, 3:4], op0=ALU.subtract, op1=ALU.mult), [ps_o, mv[k]], [on[k]])
                    c.op("pool", lambda e: e.tensor_tensor(out=rt[k][:], in0=on[k][:], in1=gt[:, n, :], op=ALU.mult), [on[k], gt], [rt[k]])
                    for hf in range(2):
                        c.op("pe", lambda e: e.transpose(out=ps_t[k][:, hf * 128:(hf + 1) * 128], in_=rt[k][:, hf * 128:(hf + 1) * 128], identity=identb[:]), [rt[k], identb], [ps_t[k]], acc=(hf == 1))
                    c.op("act", lambda e: e.copy(out=retTs[:, :, sl], in_=ps_t[k][:].rearrange("p (a b) -> p a b", a=2)), [ps_t[k]], [retTs], acc=True)
                c.dma("pool", retT[h * 256:(h + 1) * 256, :].rearrange("(a p) t -> p a t", p=128), retTs[:], [retTs], [retT], sbuf=retTs, acc=True)
            c.end_phase()


        def phase3():
            c.begin_phase()
            NTVm = NG[2] // 128
            fl = c.sb("fl", [128, 4], F32)
            eb = c.sb("eb", [128, 768], F32)
            mc = c.sb("mc", [128, 512], F32)
            c.dma("sp", fl[:], flags[:], [flags], [fl], sbuf=fl)
            c.dma("sp", mc[:], mcst[:], [mcst], [mc], sbuf=mc)
            c.dma("sp", eb[:], rel_bias[:].rearrange("a b -> (a b)").partition_broadcast(128), [rel_bias], [eb], sbuf=eb)
            c.op("act", lambda e: e.activation(out=eb[:], in_=eb[:], func=AF.Exp), [eb], [eb])
            identb = c.sb("identb", [128, 128], BF16)
            c.op("dve", lambda e: e.tensor_copy(out=identb[:], in_=mc[:, 0:128]), [mc], [identb])
            nmax = max(len(t[0]) for t in ATT_TABLES)
            am = c.sb("am", [128, nmax, 128], F32)
            E = c.sb("E", [128, 2, 128], F32)
            QT = c.sb("QT", [128, NG[2]], BF16)
            KT = c.sb("KT", [128, NG[2]], BF16)
            Va = c.sb("Va", [128, NTVm, 132], BF16)
            ou = c.sb("ou", [128, NT, 132], F32)
            c.op("pool", lambda e: e.memset(ou[:], 0.0), [], [ou])
            Pe = [c.sb("Pe%d" % k, [128, 128], F32) for k in range(4)]
            Pm = [c.sb("Pm%d" % k, [128, 128], BF16) for k in range(4)]
            ps_s = [c.ps("ps_s%d" % k, [128, 128], F32) for k in range(4)]
            ps_o = [c.ps("ps_o%d" % k, [128, 132], F32) for k in range(2)]
            ps_t = [c.ps("ps_t%d" % k, [128, 1024], BF16) for k in range(2)]
            SC = 128.0 ** -0.5
            it = 0
            for g, (_w, r) in enumerate(GROUPS):
                keys = ATT_TABLES[g][0]
                PZ = T // r + 128
                TPZ = PZ // 128
                CQ = T // r // 128
                NTV = NG[g] // 128
                c.dma("sp", am[:, 0:len(keys), :], am_in[g][:], [am_in[g]], [am], sbuf=am)
                for h in range(8):
                    gh = g * 8 + h
                    seen = set()
                    for idx, (kt, b) in enumerate(keys):
                        col = eb[:, b * 24 + gh:b * 24 + gh + 1]
                        if kt not in seen:
                            seen.add(kt)
                            c.op("dve", lambda e: e.tensor_scalar(out=E[:, kt, :], in0=am[:, idx, :], scalar1=col, scalar2=None, op0=ALU.mult), [am, eb], [E], acc=(kt == 1))
                        else:
                            c.op("dve", lambda e: e.scalar_tensor_tensor(out=E[:, kt, :], in0=am[:, idx, :], scalar=col, in1=E[:, kt, :], op0=ALU.mult, op1=ALU.add), [am, eb, E], [E], acc=True)
                    c.dma("sp", QT[:, 0:NG[g]], aqT[g][h * 128:(h + 1) * 128, :], [aqT[g]], [QT], sbuf=QT)
                    c.dma("sp", KT[:, 0:NG[g]], akT[g][h * 128:(h + 1) * 128, :], [akT[g]], [KT], sbuf=KT)
                    c.dma("sp", Va[:, 0:NTV, 0:128], av[g][:, h * 128:(h + 1) * 128].rearrange("(n p) c -> p n c", p=128), [av[g]], [Va], sbuf=Va)
                    c.op("pool", lambda e: e.memset(Va[:, 0:NTV, 128:129], 1.0), [], [Va], acc=True)
                    c.op("dve", lambda e: e.tensor_scalar(out=Va[:, 0:NTV:TPZ, 0:129], in0=Va[:, 0:NTV:TPZ, 0:129], scalar1=fl[:, 2:3], scalar2=None, op0=ALU.mult), [Va, fl], [Va], acc=True)
                    c.op("dve", lambda e: e.tensor_scalar(out=Va[:, TPZ - 1:NTV:TPZ, 0:129], in0=Va[:, TPZ - 1:NTV:TPZ, 0:129], scalar1=fl[:, 3:4], scalar2=None, op0=ALU.mult), [Va, fl], [Va], acc=True)
                    tiles = [(z, cq) for z in range(r) for cq in range(CQ)]

                    def emit_S(i):
                        z, cq = tiles[i]
                        qcol = z * PZ + 128 * cq + 64
                        for kt in range(2):
                            kcol = z * PZ + 128 * (cq + kt)
                            pS = ps_s[(i % 2) * 2 + kt]
                            c.op("pe", lambda e: e.matmul(pS[:], lhsT=KT[:, kcol:kcol + 128], rhs=QT[:, qcol:qcol + 128], start=True, stop=True), [KT, QT], [pS])

                    def emit_rest(i):
                        z, cq = tiles[i]
                        po = ps_o[i % 2]
                        for kt in range(2):
                            m = cq + kt
                            pS = ps_s[(i % 2) * 2 + kt]
                            pe_ = Pe[(i % 2) * 2 + kt]
                            pk = Pm[(i % 2) * 2 + kt]
                            c.op("act", lambda e: e.activation(out=pe_[:], in_=pS[:], func=AF.Exp, scale=SC), [pS], [pe_])
                            c.op("dve", lambda e: e.tensor_tensor(out=pk[:], in0=pe_[:], in1=E[:, kt, :], op=ALU.mult), [pe_, E], [pk])
                            c.op("pe", lambda e: e.matmul(po[:, 0:129], lhsT=pk[:], rhs=Va[:, z * TPZ + m, 0:129], start=(kt == 0), stop=(kt == 1)), [pk, Va], [po])
                        c.op("act", lambda e: e.copy(out=ou[:, z * CQ + cq, 0:129], in_=po[:, 0:129]), [po], [ou], acc=True)

                    emit_S(0)
                    for i in range(len(tiles)):
                        if i + 1 < len(tiles):
                            emit_S(i + 1)
                        emit_rest(i)
                    c.dma("pool", att_u[g][:].rearrange("(c i z) h w -> i z c h w", i=128, z=r)[:, :, :, h, :],
                          ou[:].rearrange("p (z c) w -> p z c w", z=r), [ou], [att_u[g]], sbuf=ou, acc=True)
            au = [c.sb("au%d" % k, [128, 3, 8, 132], F32) for k in range(2)]
            rden = [c.sb("rden%d" % k, [128, 8], F32) for k in range(2)]
            ab = [c.sb("ab%d" % k, [128, 8, 128], BF16) for k in range(2)]
            ats = [c.sb("ats%d" % k, [128, 8, 128], BF16) for k in range(2)]
            for n in range(NT):
                k = n % 2
                for g in range(3):
                    c.dma("sp", au[k][:, g], att_u[g][n * 128:(n + 1) * 128], [att_u[g]], [au[k]], sbuf=au[k], acc=(g > 0))
                c.op("dve", lambda e: e.tensor_tensor(out=au[k][:, 0], in0=au[k][:, 0], in1=au[k][:, 1], op=ALU.add), [au[k]], [au[k]])
                c.op("dve", lambda e: e.tensor_tensor(out=au[k][:, 0], in0=au[k][:, 0], in1=au[k][:, 2], op=ALU.add), [au[k]], [au[k]])
                c.op("dve", lambda e: e.reciprocal(out=rden[k][:], in_=au[k][:, 0, :, 128]), [au[k]], [rden[k]])
                c.op("dve", lambda e: e.tensor_tensor(out=ab[k][:], in0=au[k][:, 0, :, 0:128], in1=rden[k][:].unsqueeze(2).to_broadcast([128, 8, 128]), op=ALU.mult), [au[k], rden[k]], [ab[k]])
                for hh in range(8):
                    c.op("pe", lambda e: e.transpose(out=ps_t[k][:, hh * 128:(hh + 1) * 128], in_=ab[k][:, hh, :], identity=identb[:]), [ab[k], identb], [ps_t[k]], acc=(hh > 0))
                c.op("act", lambda e: e.copy(out=ats[k][:], in_=ps_t[k][:].rearrange("p (a b) -> p a b", a=8)), [ps_t[k]], [ats[k]])
                c.dma("pool", attT[:, n * 128:(n + 1) * 128].rearrange("(a p) t -> p a t", p=128), ats[k][:], [ats[k]], [attT], sbuf=ats[k], acc=True)
            c.end_phase()

        def load_weight_bf16(wd, nk, dst, wst, ncols=D, q0=0):
            for i, c0 in enumerate(range(0, ncols, 256)):
                k = i % 2
                c.dma("sp", wst[k][:, 0:nk, :], wd[:, c0:c0 + 256].rearrange("(kc p) n -> p kc n", p=128), [wd], [wst[k]], sbuf=wst[k])
                eng = "dve" if k == 0 else "pool"
                c.op(eng, lambda e: e.tensor_copy(out=dst[:, 0:nk, q0 + c0:q0 + c0 + 256], in_=wst[k][:, 0:nk, :]), [wst[k]], [dst], acc=True)

        def layernorm(src, dst, gam, bet, st, mv):
            for q in range(4):
                c.op("dve", lambda e: e.bn_stats(out=st[:, q * 6:(q + 1) * 6], in_=src[:, q * 512:(q + 1) * 512]), [src], [st], acc=(q > 0))
            c.op("dve", lambda e: e.bn_aggr(out=mv[:, 0:2], in_=st[:, 0:24]), [st], [mv])
            c.op("dve", lambda e: e.tensor_scalar(out=mv[:, 2:3], in0=mv[:, 1:2], scalar1=LN_EPS, scalar2=None, op0=ALU.add), [mv], [mv])
            c.op("act", lambda e: e.sqrt(out=mv[:, 2:3], in_=mv[:, 2:3]), [mv], [mv])
            c.op("dve", lambda e: e.reciprocal(out=mv[:, 3:4], in_=mv[:, 2:3]), [mv], [mv])
            c.op("dve", lambda e: e.tensor_scalar(out=dst[:], in0=src[:], scalar1=mv[:, 0:1], scalar2=mv[:, 3:4], op0=ALU.subtract, op1=ALU.mult), [src, mv], [dst])
            c.op("pool", lambda e: e.tensor_tensor(out=dst[:], in0=dst[:], in1=gam[:], op=ALU.mult), [dst, gam], [dst])
            c.op("pool", lambda e: e.tensor_tensor(out=dst[:], in0=dst[:], in1=bet[:], op=ALU.add), [dst, bet], [dst])

        def phase4a():
            c.begin_phase()
            wst = [c.sb("wst%d" % k, [128, KC, 256], F32) for k in range(2)]
            Wro = c.sb("Wro", [128, KC, D], BF16)
            Wao = c.sb("Wao", [128, 8, D], BF16)
            load_weight_bf16(w_ret_o, KC, Wro, wst)
            load_weight_bf16(w_att_o, 8, Wao, wst)
            rT = [c.sb("rT%d" % k, [128, KC, 512], BF16) for k in range(2)]
            aT = [c.sb("aT%d" % k, [128, 8, 512], BF16) for k in range(2)]
            ga = [c.sb("ga%d" % k, [128, 512], BF16) for k in range(2)]
            gb = [c.sb("gb%d" % k, [128, 512], BF16) for k in range(2)]
            m1 = [c.sb("m1%d" % k, [128, 512], F32) for k in range(2)]
            m2 = [c.sb("m2%d" % k, [128, 512], F32) for k in range(2)]
            mo = [c.sb("mo%d" % k, [128, 512], BF16) for k in range(2)]
            ps1 = [c.ps("ps1%d" % k, [128, 512], F32) for k in range(2)]
            ps2 = [c.ps("ps2%d" % k, [128, 512], F32) for k in range(2)]
            it = 0
            for tb in range(T // 512):
                kb = tb % 2
                ts_ = slice(tb * 512, (tb + 1) * 512)
                c.dma("sp", rT[kb][:], retT[:, ts_].rearrange("(kc p) t -> p kc t", p=128), [retT], [rT[kb]], sbuf=rT[kb])
                c.dma("sp", aT[kb][:], attT[:, ts_].rearrange("(kc p) t -> p kc t", p=128), [attT], [aT[kb]], sbuf=aT[kb])
                for cc in range(KC):
                    k = it % 2
                    it += 1
                    cs_ = slice(cc * 128, (cc + 1) * 128)
                    c.dma("sp", ga[k][:], sgaT[cs_, ts_], [sgaT], [ga[k]], sbuf=ga[k])
                    c.dma("sp", gb[k][:], sgbT[cs_, ts_], [sgbT], [gb[k]], sbuf=gb[k])
                    for kc in range(KC):
                        c.op("pe", lambda e: e.matmul(ps1[k][:], lhsT=Wro[:, kc, cs_], rhs=rT[kb][:, kc, :], start=(kc == 0), stop=(kc == KC - 1)), [Wro, rT[kb]], [ps1[k]])
                    for kc in range(8):
                        c.op("pe", lambda e: e.matmul(ps2[k][:], lhsT=Wao[:, kc, cs_], rhs=aT[kb][:, kc, :], start=(kc == 0), stop=(kc == 7)), [Wao, aT[kb]], [ps2[k]])
                    c.op("dve", lambda e: e.tensor_tensor(out=m1[k][:], in0=ps1[k][:], in1=ga[k][:], op=ALU.mult), [ps1[k], ga[k]], [m1[k]])
                    c.op("dve", lambda e: e.tensor_tensor(out=m2[k][:], in0=ps2[k][:], in1=gb[k][:], op=ALU.mult), [ps2[k], gb[k]], [m2[k]])
                    c.op("pool", lambda e: e.tensor_tensor(out=mo[k][:], in0=m1[k][:], in1=m2[k][:], op=ALU.add), [m1[k], m2[k]], [mo[k]])
                    c.dma("pool", mergedT[cs_, ts_], mo[k][:], [mo[k]], [mergedT], sbuf=mo[k], acc=True)
            c.end_phase()

        def phase4b():
            c.begin_phase()
            wst = [c.sb("wst%d" % k, [128, KC, 256], F32) for k in range(2)]
            Wo = c.sb("Wo", [128, KC, D], BF16)
            load_weight_bf16(w_out, KC, Wo, wst)
            mc = c.sb("mc", [128, 512], F32)
            c.dma("sp", mc[:], mcst[:], [mcst], [mc], sbuf=mc)
            identb = c.sb("identb", [128, 128], BF16)
            c.op("dve", lambda e: e.tensor_copy(out=identb[:], in_=mc[:, 0:128]), [mc], [identb])
            gam, bet = c.sb("gam", [128, D], F32), c.sb("bet", [128, D], F32)
            c.dma("sp", gam[:], ln1_g[:].rearrange("a b -> (a b)").partition_broadcast(128), [ln1_g], [gam], sbuf=gam)
            c.dma("sp", bet[:], ln1_b[:].rearrange("a b -> (a b)").partition_broadcast(128), [ln1_b], [bet], sbuf=bet)
            mt = [c.sb("mt%d" % k, [128, KC, 128], BF16) for k in range(2)]
            xt = [c.sb("xt%d" % k, [128, D], F32) for k in range(2)]
            y1 = [c.sb("y1%d" % k, [128, D], F32) for k in range(2)]
            hh = [c.sb("hh%d" % k, [128, D], F32) for k in range(2)]
            hb = [c.sb("hb%d" % k, [128, D], BF16) for k in range(2)]
            hTs = [c.sb("hTs%d" % k, [128, KC, 128], BF16) for k in range(2)]
            st = [c.sb("st%d" % k, [128, 24], F32) for k in range(2)]
            mv = [c.sb("mv%d" % k, [128, 4], F32) for k in range(2)]
            pss = [c.ps("pss%d" % k, [128, 512], F32) for k in range(4)]
            ps_t = [c.ps("ps_t%d" % k, [128, 1024], BF16) for k in range(2)]
            for n in range(NT):
                k = n % 2
                sl = slice(n * 128, (n + 1) * 128)
                c.dma("sp", mt[k][:], mergedT[:, sl].rearrange("(kc p) t -> p kc t", p=128), [mergedT], [mt[k]], sbuf=mt[k])
                c.dma("sp", xt[k][:], x_tm[sl, :], [x_tm], [xt[k]], sbuf=xt[k])
                for nb in range(4):
                    for kc in range(KC):
                        c.op("pe", lambda e: e.matmul(pss[nb][:], lhsT=mt[k][:, kc, :], rhs=Wo[:, kc, nb * 512:(nb + 1) * 512], start=(kc == 0), stop=(kc == KC - 1)), [mt[k], Wo], [pss[nb]])
                    c.op("dve", lambda e: e.scalar_tensor_tensor(out=y1[k][:, nb * 512:(nb + 1) * 512], in0=xt[k][:, nb * 512:(nb + 1) * 512], scalar=DN_ALPHA, in1=pss[nb][:], op0=ALU.mult, op1=ALU.add), [xt[k], pss[nb]], [y1[k]], acc=(nb > 0))
                layernorm(y1[k], hh[k], gam, bet, st[k], mv[k])
                c.dma("pool", h_tm[sl, :], hh[k][:], [hh[k]], [h_tm], sbuf=hh[k], acc=True)
                c.op("act", lambda e: e.copy(out=hb[k][:], in_=hh[k][:]), [hh[k]], [hb[k]])
                for half in range(2):
                    for q in range(8):
                        kc = half * 8 + q
                        c.op("pe", lambda e: e.transpose(out=ps_t[half][:, q * 128:(q + 1) * 128], in_=hb[k][:, kc * 128:(kc + 1) * 128], identity=identb[:]), [hb[k], identb], [ps_t[half]], acc=(q > 0))
                    c.op("act", lambda e: e.copy(out=hTs[k][:, half * 8:(half + 1) * 8, :], in_=ps_t[half][:].rearrange("p (a b) -> p a b", a=8)), [ps_t[half]], [hTs[k]], acc=(half > 0))
                c.dma("pool", hT[:].rearrange("kc p t -> p kc t")[:, :, sl], hTs[k][:], [hTs[k]], [hT], sbuf=hTs[k], acc=True)
            c.end_phase()


        def phase5a():
            c.begin_phase()
            U32 = mybir.dt.uint32
            wst = [c.sb("wst%d" % k, [128, KC, 256], F32) for k in range(2)]
            Wq = c.sb("Wq", [128, KC, D], BF16)
            load_weight_bf16(peer_wq, KC, Wq, wst)
            mc = c.sb("mc", [128, 512], F32)
            c.dma("sp", mc[:], mcst[:], [mcst], [mc], sbuf=mc)
            kst = c.sb("kst", [128, 16, 128], F32)
            kTb = c.sb("kTb", [128, 16, 128], BF16)
            c.dma("sp", kst[:], keysT[:].rearrange("a d k -> d a k"), [keysT], [kst], sbuf=kst)
            c.op("dve", lambda e: e.tensor_copy(out=kTb[:], in_=kst[:]), [kst], [kTb])
            hTb = [c.sb("hTb0", [128, KC, 512], BF16)] * 2
            qT = c.sb("qT", [128, 16, 512], BF16)
            ssb = [c.sb("ssb%d" % k, [128, 16, 128], F32) for k in range(2)]
            t16 = c.sb("t16", [128, 8, 2, 16], F32)
            ix = c.sb("ix", [128, 8, 2, 16], U32)
            ixf = c.sb("ixf", [128, 8, 2, 16], F32)
            wk = c.sb("wk", [128, 128], F32)
            cand = c.sb("cand", [128, 8, 16, 16], F32)
            wk2 = c.sb("wk2", [128, 256], F32)
            b16 = c.sb("b16", [128, 8, 16], F32)
            px = c.sb("px", [128, 8, 16], U32)
            pabf = c.sb("pabf", [128, 2, 8, 16], F32)
            oh = c.sb("oh", [128, 8, 16, 16], F32)
            e16 = c.sb("e16", [128, 8, 16], F32)
            sm = c.sb("sm", [128, 16], F32)
            IJg = [c.sb("IJg%d" % k, [128, 3, 128], F32) for k in range(2)]
            IJgT = [c.sb("IJgT%d" % k, [128, 3, 128], F32) for k in range(2)]
            psq = [c.ps("psq%d" % k, [128, 512], F32) for k in range(2)]
            pss = [c.ps("pss%d" % k, [128, 512], F32) for k in range(2)]
            pst = [c.ps("pst%d" % k, [128, 384], F32) for k in range(2)]
            io16 = mc[:, 384:400]
            identf = mc[:, 0:128]
            ubf = [c.sb("ubf%d" % k, [128, KC, 128], BF16) for k in range(2)]
            vbf = [c.sb("vbf%d" % k, [128, D], BF16) for k in range(2)]
            tj = [0]

            def table_iter():
                j = tj[0]
                if j >= 128:
                    return
                tj[0] += 1
                k = j % 2
                ust = wst[k][:, 0:8, :].rearrange("p a b -> p (a b)").rearrange("p (dc i) -> p dc i", i=128)
                vst = wst[k][:, 8:16, :].rearrange("p a b -> p (a b)")
                c.dma("sp", ust, uT[j].rearrange("dc p i -> p dc i"), [uT], [wst[k]], sbuf=wst[k])
                c.dma("sp", vst, vj[j], [vj], [wst[k]], sbuf=wst[k], par=True)
                c.op("act", lambda e: e.copy(out=ubf[k][:], in_=ust), [wst[k]], [ubf[k]])
                c.op("act", lambda e: e.copy(out=vbf[k][:], in_=vst), [wst[k]], [vbf[k]])
                c.dma("pool", uTb[j], ubf[k][:], [ubf[k]], [uTb], sbuf=ubf[k], acc=True)
                c.dma("pool", vb[j], vbf[k][:], [vbf[k]], [vb], sbuf=vbf[k], acc=True)

            n_tbl = -(-128 // (T // 128))
            it = 0
            for tb in range(T // 512):
                kb = tb % 2
                c.dma("sp", hTb[kb][:], hT[:].rearrange("kc p t -> p kc t")[:, :, tb * 512:(tb + 1) * 512], [hT], [hTb[kb]], sbuf=hTb[kb])
                for hc in range(16):
                    p = psq[hc % 2]
                    for kc in range(KC):
                        c.op("pe", lambda e: e.matmul(p[:], lhsT=Wq[:, kc, hc * 128:(hc + 1) * 128], rhs=hTb[kb][:, kc, :], start=(kc == 0), stop=(kc == KC - 1)), [Wq, hTb[kb]], [p])
                    c.op("act", lambda e: e.copy(out=qT[:, hc, :], in_=p[:]), [p], [qT], acc=True)
                for tt in range(4):
                    k = it % 2
                    it += 1
                    tsl = slice(tt * 128, (tt + 1) * 128)
                    for _ in range(n_tbl):
                        table_iter()
                    for qd in range(4):
                        p = pss[qd % 2]
                        for jj in range(4):
                            hc = qd * 4 + jj
                            c.op("pe", lambda e: e.matmul(p[:, jj * 128:(jj + 1) * 128], lhsT=qT[:, hc, tsl], rhs=kTb[:, hc, :], start=True, stop=True), [qT, kTb], [p], acc=(jj > 0))
                        c.op("act", lambda e: e.copy(out=ssb[k][:, qd * 4:(qd + 1) * 4, :], in_=p[:].rearrange("p (a b) -> p a b", a=4)), [p], [ssb[k]], acc=(qd > 0))
                    X = mybir.AxisListType.X
                    for h in range(8):
                        for cc in range(2):
                            sv = ssb[k][:, 2 * h + cc, :]
                            c.op("dve", lambda e: e.max(out=t16[:, h, cc, 0:8], in_=sv), [ssb[k]], [t16], acc=True)
                            c.op("dve", lambda e: e.max_index(out=ix[:, h, cc, 0:8], in_max=t16[:, h, cc, 0:8], in_values=sv), [ssb[k], t16], [ix], acc=True)
                            c.op("dve", lambda e: e.match_replace(out=wk[:], in_to_replace=t16[:, h, cc, 0:8], in_values=sv, imm_value=-1e30), [ssb[k], t16], [wk])
                            c.op("dve", lambda e: e.max(out=t16[:, h, cc, 8:16], in_=wk[:]), [wk], [t16], acc=True)
                            c.op("dve", lambda e: e.max_index(out=ix[:, h, cc, 8:16], in_max=t16[:, h, cc, 8:16], in_values=wk[:]), [wk, t16], [ix], acc=True)
                    c.op("dve", lambda e: e.tensor_copy(out=ixf[:], in_=ix[:]), [ix], [ixf])
                    c.op("dve", lambda e: e.tensor_tensor(out=cand[:], in0=t16[:, :, 0, :].unsqueeze(3).to_broadcast([128, 8, 16, 16]), in1=t16[:, :, 1, :].unsqueeze(2).to_broadcast([128, 8, 16, 16]), op=ALU.add), [t16], [cand])
                    for h in range(8):
                        cflat = cand[:, h].rearrange("p a b -> p (a b)")
                        c.op("dve", lambda e: e.max(out=b16[:, h, 0:8], in_=cflat), [cand], [b16], acc=True)
                        c.op("dve", lambda e: e.max_index(out=px[:, h, 0:8], in_max=b16[:, h, 0:8], in_values=cflat), [cand, b16], [px], acc=True)
                        c.op("dve", lambda e: e.match_replace(out=wk2[:], in_to_replace=b16[:, h, 0:8], in_values=cflat, imm_value=-1e30), [cand, b16], [wk2])
                        c.op("dve", lambda e: e.max(out=b16[:, h, 8:16], in_=wk2[:]), [wk2], [b16], acc=True)
                        c.op("dve", lambda e: e.max_index(out=px[:, h, 8:16], in_max=b16[:, h, 8:16], in_values=wk2[:]), [wk2, b16], [px], acc=True)
                    c.op("dve", lambda e: e.tensor_copy(out=e16[:], in_=px[:]), [px], [e16])
                    c.op("dve", lambda e: e.tensor_tensor(out=oh[:], in0=e16[:].unsqueeze(3).to_broadcast([128, 8, 16, 16]), in1=mc[:, 400:416].unsqueeze(1).unsqueeze(1).to_broadcast([128, 8, 16, 16]), op=ALU.is_ge), [e16, mc], [oh])
                    c.op("dve", lambda e: e.tensor_reduce(out=pabf[:, 0], in_=oh[:], axis=X, op=ALU.add), [oh], [pabf], acc=True)
                    c.op("dve", lambda e: e.scalar_tensor_tensor(out=pabf[:, 1], in0=pabf[:, 0], scalar=-16.0, in1=e16[:], op0=ALU.mult, op1=ALU.add), [e16, pabf], [pabf], acc=True)
                    for cc in range(2):
                        c.op("dve", lambda e: e.tensor_tensor(out=oh[:], in0=pabf[:, cc].unsqueeze(3).to_broadcast([128, 8, 16, 16]), in1=io16.unsqueeze(1).unsqueeze(1).to_broadcast([128, 8, 16, 16]), op=ALU.is_equal), [pabf, mc], [oh])
                        c.op("dve", lambda e: e.tensor_tensor(out=oh[:], in0=oh[:], in1=ixf[:, :, cc, :].unsqueeze(2).to_broadcast([128, 8, 16, 16]), op=ALU.mult), [oh, ixf], [oh])
                        c.op("dve", lambda e: e.tensor_reduce(out=IJg[k][:, cc, :].rearrange("p (h k) -> p h k", h=8), in_=oh[:], axis=X, op=ALU.add), [oh], [IJg[k]], acc=True)
                    c.op("dve", lambda e: e.tensor_tensor(out=e16[:], in0=b16[:], in1=b16[:, :, 0:1].to_broadcast([128, 8, 16]), op=ALU.subtract), [b16], [e16])
                    c.op("act", lambda e: e.activation(out=e16[:], in_=e16[:], func=AF.Exp), [e16], [e16])
                    c.op("dve", lambda e: e.tensor_reduce(out=sm[:, 0:8], in_=e16[:], axis=X, op=ALU.add), [e16], [sm])
                    c.op("dve", lambda e: e.reciprocal(out=sm[:, 8:16], in_=sm[:, 0:8]), [sm], [sm])
                    c.op("dve", lambda e: e.tensor_tensor(out=IJg[k][:, 2, :].rearrange("p (h k) -> p h k", h=8), in0=e16[:], in1=sm[:, 8:16].unsqueeze(2).to_broadcast([128, 8, 16]), op=ALU.mult), [e16, sm], [IJg[k]], acc=True)
                    for q in range(3):
                        c.op("pe", lambda e: e.transpose(out=pst[k][:, q * 128:(q + 1) * 128], in_=IJg[k][:, q, :], identity=identf), [IJg[k], mc], [pst[k]], acc=(q > 0))
                    c.op("act", lambda e: e.copy(out=IJgT[k][:], in_=pst[k][:].rearrange("p (a b) -> p a b", a=3)), [pst[k]], [IJgT[k]])
                    t0 = tb * 512 + tt * 128
                    c.dma("pool", IT[:, t0:t0 + 128], IJgT[k][:, 0, :], [IJgT[k]], [IT], sbuf=IJgT[k], acc=True)
                    c.dma("pool", JT[:, t0:t0 + 128], IJgT[k][:, 1, :], [IJgT[k]], [JT], sbuf=IJgT[k], acc=True)
                    c.dma("pool", gT[:, t0:t0 + 128], IJgT[k][:, 2, :], [IJgT[k]], [gT], sbuf=IJgT[k], acc=True)
            c.end_phase()

        def phase5b():
            c.begin_phase()
            wst = [c.sb("wst%d" % k, [128, KC, 256], F32) for k in range(2)]
            Wpg = c.sb("Wpg", [128, KC, D], BF16)
            Wpe = c.sb("Wpe", [128, 2, D], BF16)
            load_weight_bf16(w_pg, KC, Wpg, wst)
            load_weight_bf16(w_pe, 2, Wpe, wst)
            hTt = [c.sb("hTt%d" % k, [128, KC, 128], BF16) for k in range(2)]
            pTt = [c.sb("pTt%d" % k, [128, 2, 128], BF16) for k in range(2)]
            ht = [c.sb("ht%d" % k, [128, D], F32) for k in range(2)]
            bs = [c.sb("bs%d" % k, [128, D], F32) for k in range(2)]
            sg = [c.sb("sg%d" % k, [128, 512], F32) for k in range(2)]
            pe_ = [c.sb("pe_%d" % k, [128, 512], F32) for k in range(2)]
            ps1 = [c.ps("ps1%d" % k, [128, 512], F32) for k in range(2)]
            ps2 = [c.ps("ps2%d" % k, [128, 512], F32) for k in range(2)]
            it = 0
            for n in range(NT):
                k = n % 2
                sl = slice(n * 128, (n + 1) * 128)
                c.dma("sp", hTt[k][:], hT[:].rearrange("kc p t -> p kc t")[:, :, sl], [hT], [hTt[k]], sbuf=hTt[k])
                c.dma("sp", pTt[k][:], pb[:].rearrange("kc p t -> p kc t")[:, :, sl], [pb], [pTt[k]], sbuf=pTt[k])
                c.dma("sp", ht[k][:], h_tm[sl, :], [h_tm], [ht[k]], sbuf=ht[k])
                for nb in range(4):
                    q = it % 2
                    it += 1
                    ns = slice(nb * 512, (nb + 1) * 512)
                    for kc in range(KC):
                        c.op("pe", lambda e: e.matmul(ps1[q][:], lhsT=hTt[k][:, kc, :], rhs=Wpg[:, kc, ns], start=(kc == 0), stop=(kc == KC - 1)), [hTt[k], Wpg], [ps1[q]])
                    for kc in range(2):
                        c.op("pe", lambda e: e.matmul(ps2[q][:], lhsT=pTt[k][:, kc, :], rhs=Wpe[:, kc, ns], start=(kc == 0), stop=(kc == 1)), [pTt[k], Wpe], [ps2[q]])
                    c.op("act", lambda e: e.activation(out=sg[q][:], in_=ps1[q][:], func=AF.Sigmoid), [ps1[q]], [sg[q]])
                    c.op("dve", lambda e: e.tensor_tensor(out=pe_[q][:], in0=ps2[q][:], in1=sg[q][:], op=ALU.mult), [ps2[q], sg[q]], [pe_[q]])
                    c.op("dve", lambda e: e.scalar_tensor_tensor(out=bs[k][:, ns], in0=ht[k][:, ns], scalar=DN_ALPHA, in1=pe_[q][:], op0=ALU.mult, op1=ALU.add), [ht[k], pe_[q]], [bs[k]], acc=(nb > 0))
                c.dma("pool", base[sl, :], bs[k][:], [bs[k]], [base], sbuf=bs[k], acc=True)
            c.end_phase()

        def phase5c():
            c.begin_phase()
            ust = [c.sb("ust%d" % k, [128, KC, 128], F32) for k in range(2)]
            ubf = [c.sb("ubf%d" % k, [128, KC, 128], BF16) for k in range(2)]
            vst = [c.sb("vst%d" % k, [128, D], F32) for k in range(2)]
            vbf = [c.sb("vbf%d" % k, [128, D], BF16) for k in range(2)]
            for j in range(128):
                k = j % 2
                c.dma("sp", ust[k][:], uT[j].rearrange("dc p i -> p dc i"), [uT], [ust[k]], sbuf=ust[k])
                c.op("act", lambda e: e.copy(out=ubf[k][:], in_=ust[k][:]), [ust[k]], [ubf[k]])
                c.dma("pool", uTb[j], ubf[k][:], [ubf[k]], [uTb], sbuf=ubf[k], acc=True)
                c.dma("sp", vst[k][:], vj[j], [vj], [vst[k]], sbuf=vst[k])
                c.op("dve", lambda e: e.tensor_copy(out=vbf[k][:, 0:1024], in_=vst[k][:, 0:1024]), [vst[k]], [vbf[k]])
                c.op("pool", lambda e: e.tensor_copy(out=vbf[k][:, 1024:D], in_=vst[k][:, 1024:D]), [vst[k]], [vbf[k]], acc=True)
                c.dma("pool", vb[j], vbf[k][:], [vbf[k]], [vb], sbuf=vbf[k], acc=True)
            c.end_phase()

        def phase5d():
            c.begin_phase()
            TB = 256
            mc = c.sb("mc", [128, 512], F32)
            c.dma("sp", mc[:], mcst[:], [mcst], [mc], sbuf=mc)
            iotab = c.sb("iotab", [128, 128], BF16)
            c.op("dve", lambda e: e.tensor_copy(out=iotab[:], in_=mc[:, 256:384]), [mc], [iotab])
            iota = iotab[:]
            gam, bet = c.sb("gam", [128, D], F32), c.sb("bet", [128, D], F32)
            c.dma("sp", gam[:], ln2_g[:].rearrange("a b -> (a b)").partition_broadcast(128), [ln2_g], [gam], sbuf=gam)
            c.dma("sp", bet[:], ln2_b[:].rearrange("a b -> (a b)").partition_broadcast(128), [ln2_b], [bet], sbuf=bet)
            A = c.sb("A", [128, 128, TB], BF16)
            hTb = [c.sb("hTb%d" % k, [128, KC, TB], BF16) for k in range(2)]
            itj = [c.sb("itj%d" % k, [128, 3, TB], F32) for k in range(2)]
            itjb = [c.sb("itjb%d" % k, [128, 3, TB], BF16) for k in range(2)]
            NPF = 5
            TBW = 16
            ub = [c.sb("ub%d" % k, [128, KC, 128], BF16) for k in range(NPF)]
            vbt = [c.sb("vbt%d" % k, [128, D], BF16) for k in range(NPF)]
            OI = [c.sb("OI%d" % k, [128, TBW, 128], BF16) for k in range(2)]
            OJ = [c.sb("OJ%d" % k, [128, TBW, 128], BF16) for k in range(2)]
            bs = [c.sb("bs0", [128, D], F32)] * 2
            yo = [c.sb("yo0", [128, D], F32)] * 2
            yn = yo
            st = [c.sb("st%d" % k, [128, 24], F32) for k in range(2)]
            mv = [c.sb("mv%d" % k, [128, 4], F32) for k in range(2)]
            pbk = [c.ps("pbk%d" % k, [128, 512], F32) for k in range(8)]
            for blk in range(T // TB):
                kb = blk % 2
                bsl = slice(blk * TB, (blk + 1) * TB)
                c.dma("sp", hTb[kb][:], hT[:].rearrange("kc p t -> p kc t")[:, :, bsl], [hT], [hTb[kb]], sbuf=hTb[kb])
                c.dma("sp", itj[kb][:, 0, :], IT[:, bsl], [IT], [itj[kb]], sbuf=itj[kb])
                c.dma("sp", itj[kb][:, 1, :], JT[:, bsl], [JT], [itj[kb]], sbuf=itj[kb], acc=True)
                c.dma("sp", itj[kb][:, 2, :], gT[:, bsl], [gT], [itj[kb]], sbuf=itj[kb], acc=True)
                for j in range(128):
                    ku = j % NPF
                    p = pbk[j % 2]
                    c.dma("sp", ub[ku][:], uTb[j], [uTb], [ub[ku]], sbuf=ub[ku])
                    for kc in range(KC):
                        c.op("pe", lambda e: e.matmul(p[:, 0:TB], lhsT=ub[ku][:, kc, :], rhs=hTb[kb][:, kc, :], start=(kc == 0), stop=(kc == KC - 1)), [ub[ku], hTb[kb]], [p])
                    c.op("act", lambda e: e.activation(out=A[:, j, :], in_=p[:, 0:TB], func=AF.Gelu), [p], [A], acc=True)
                c.op("dve", lambda e: e.tensor_copy(out=itjb[kb][:], in_=itj[kb][:]), [itj[kb]], [itjb[kb]])
                for tb0 in range(0, TB, TBW):
                    kq = (tb0 // TBW) % 2
                    io_b = iota.unsqueeze(1).to_broadcast([128, TBW, 128])
                    c.op("dve", lambda e: e.tensor_tensor(out=OI[kq][:], in0=io_b, in1=itjb[kb][:, 0, tb0:tb0 + TBW].unsqueeze(2).to_broadcast([128, TBW, 128]), op=ALU.is_equal), [iotab, itjb[kb]], [OI[kq]])
                    c.op("dve", lambda e: e.tensor_tensor(out=OI[kq][:], in0=OI[kq][:], in1=itjb[kb][:, 2, tb0:tb0 + TBW].unsqueeze(2).to_broadcast([128, TBW, 128]), op=ALU.mult), [OI[kq], itjb[kb]], [OI[kq]])
                    c.op("dve", lambda e: e.tensor_tensor(out=OJ[kq][:], in0=io_b, in1=itjb[kb][:, 1, tb0:tb0 + TBW].unsqueeze(2).to_broadcast([128, TBW, 128]), op=ALU.is_equal), [iotab, itjb[kb]], [OJ[kq]])
                    for t4 in range(0, TBW, 4):
                        pw = pbk[2 + (t4 // 4) % 2]
                        for q in range(4):
                            c.op("pe", lambda e: e.matmul(pw[:, q * 128:(q + 1) * 128], lhsT=OI[kq][:, t4 + q, :], rhs=OJ[kq][:, t4 + q, :], start=True, stop=True), [OI[kq], OJ[kq]], [pw], acc=(q > 0))
                        t = tb0 + t4
                        c.op("dve", lambda e: e.tensor_tensor(out=A[:, :, t:t + 4], in0=pw[:].rearrange("p (q j) -> p j q", q=4), in1=A[:, :, t:t + 4], op=ALU.mult), [pw, A], [A], acc=True)
                for j in range(128):
                    kv = j % NPF
                    c.dma("sp", vbt[kv][:], vb[j], [vb], [vbt[kv]], sbuf=vbt[kv])
                    for tt in range(2):
                        for nb in range(4):
                            c.op("pe", lambda e: e.matmul(pbk[tt * 4 + nb][:], lhsT=A[:, j, tt * 128:(tt + 1) * 128], rhs=vbt[kv][:, nb * 512:(nb + 1) * 512], start=(j == 0), stop=(j == 127)), [A, vbt[kv]], [pbk[tt * 4 + nb]])
                for tt in range(2):
                    k = tt
                    rs_ = slice(blk * TB + tt * 128, blk * TB + (tt + 1) * 128)
                    c.dma("sp", bs[k][:], base[rs_, :], [base], [bs[k]], sbuf=bs[k])
                    for nb in range(4):
                        ns = slice(nb * 512, (nb + 1) * 512)
                        c.op("dve", lambda e: e.tensor_tensor(out=yo[k][:, ns], in0=pbk[tt * 4 + nb][:], in1=bs[k][:, ns], op=ALU.add), [pbk[tt * 4 + nb], bs[k]], [yo[k]], acc=(nb > 0))
                    layernorm(yo[k], yn[k], gam, bet, st[k], mv[k])
                    c.dma("pool", y[rs_, :], yn[k][:], [yn[k]], [y], sbuf=yn[k], acc=True)
            c.end_phase()

        phases = [phase1, phaseA, phase2, phase3, phase4a, phase4b, phase5a, phase5b, phase5d]
        for i, ph in enumerate(phases):
            if i < upto:
                ph()
        c.barrier()
    return nc


def shared_inputs(T, inp):
    sh = {
        "rcst": ret_consts(T), "mcst": misc_consts(),
        "w_in": np.ascontiguousarray(inp["w_in"][0]),
        "ret_decay_logit": np.ascontiguousarray(inp["ret_decay_logit"][0]),
        "w_ret_o": np.ascontiguousarray(inp["w_ret_o"][0]), "w_att_o": np.ascontiguousarray(inp["w_att_o"][0]),
        "w_out": np.ascontiguousarray(inp["w_out"][0]), "rel_bias": np.ascontiguousarray(inp["rel_bias"]),
        "ln1_g": np.ascontiguousarray(inp["ln1_g"]), "ln1_b": np.ascontiguousarray(inp["ln1_b"]),
        "ln2_g": np.ascontiguousarray(inp["ln2_g"]), "ln2_b": np.ascontiguousarray(inp["ln2_b"]),
        "peer_wq": np.ascontiguousarray(inp["peer_wq"][0]),
        "keysT": np.ascontiguousarray(inp["peer_keys"][0].reshape(16, 128, 128).transpose(0, 2, 1)),
        "uT": np.ascontiguousarray(inp["peer_u"][0].reshape(128, 128, KC, 128).transpose(1, 2, 3, 0)),
        "vj": np.ascontiguousarray(inp["peer_v"][0].reshape(128, 128, D).transpose(1, 0, 2)),
        "w_pe": np.ascontiguousarray(inp["w_pe"][0]), "w_pg": np.ascontiguousarray(inp["w_pg"][0]),
    }
    for g in range(3):
        sh["am%d" % g] = ATT_TABLES[g][1]
    return sh


def core_inputs(x_seq, p_seq, s0, T, sh):
    S = x_seq.shape[0]
    has_l, has_r = s0 > 0, s0 + T < S
    ext = np.zeros((T + 2 * HALO, D), np.float32)
    ext[HALO:HALO + T] = x_seq[s0:s0 + T]
    nbr = np.zeros((T, D), np.float32)
    pos_n = np.arange(T)
    if has_l:
        ext[:HALO] = x_seq[s0 - HALO:s0]
        nbr = x_seq[s0 - T:s0]
        pos_n = np.arange(s0 - T, s0)
    if has_r:
        ext[HALO + T:] = x_seq[s0 + T:s0 + T + HALO]
        nbr = x_seq[s0 + T:s0 + 2 * T]
        pos_n = np.arange(s0 + T, s0 + 2 * T)
    fl = np.zeros((128, 4), np.float32)
    fl[:, 0], fl[:, 1] = float(has_l), float(has_r)
    fl[:, 2], fl[:, 3] = 1.0, 1.0
    fl[:64, 2] = float(has_l)
    fl[64:, 3] = float(has_r)
    m = dict(sh)
    m.update({
        "xT_ext": np.ascontiguousarray(ext.T), "xT_nbr": np.ascontiguousarray(nbr.T),
        "x_tm": np.ascontiguousarray(x_seq[s0:s0 + T]), "pT": np.ascontiguousarray(p_seq[s0:s0 + T].T),
        "flags": fl, "cs_own": rot_table(np.arange(s0, s0 + T)), "cs_nbr": rot_table(pos_n),
    })
    return m


_NC_CACHE = {}


def kernel(**inputs):
    T = 4096
    inp = {k: np.asarray(v) for k, v in inputs.items()}
    sh = shared_inputs(T, inp)
    in_maps = []
    for b in range(4):
        in_maps.append(core_inputs(inp["x_prompt"][b], inp["p_prompt"][0, b], 0, T, sh))
    for b in range(2):
        for hf in range(2):
            in_maps.append(core_inputs(inp["x_sample"][b], inp["p_sample"][0, b], hf * T, T, sh))
    if T not in _NC_CACHE:
        _NC_CACHE[T] = build(T)
    res = run_bass_kernel_spmd(_NC_CACHE[T], in_maps, core_ids=list(range(8)))
    ys = [np.asarray(r["y"], np.float32) for r in res.results]
    y_prompt = np.stack(ys[0:4], 0)
    y_sample = np.stack([np.concatenate(ys[4:6], 0), np.concatenate(ys[6:8], 0)], 0)
    return (y_prompt, y_sample)
```

```python
import contextlib
import math
import numpy as np
import concourse.bass as bass
import concourse.mybir as mybir
from concourse.bass_utils import run_bass_kernel_spmd

F32 = mybir.dt.float32
BF16 = mybir.dt.bfloat16
AF = mybir.ActivationFunctionType
ALU = mybir.AluOpType

D = 2048
KC = 16
IN_W = 19456
HALO = 1024
GROUPS = ((128, 1), (512, 4), (2048, 16))
DN_ALPHA = 2.0 ** 0.25
LN_EPS = 1e-5
NKEY = 128


class DSem:
    def __init__(self, sem):
        self.sem = sem
        self.cnt = 0


class Buf:
    def __init__(self, ctx, t, name):
        self.ctx, self.t, self.name = ctx, t, name
        self.w, self.r, self.rp = {}, {}, {}
        self.ds = None

    def __getitem__(self, idx):
        return self.t[idx]

    def dsem(self):
        if self.ds is None:
            self.ds = self.ctx.get_dsem()
        return self.ds


class Ctx:
    def __init__(self, nc, es):
        self.nc, self.es = nc, es
        self.eng = {"pe": nc.tensor, "act": nc.scalar, "dve": nc.vector, "pool": nc.gpsimd, "sp": nc.sync}
        self.sem, self.cnt, self.known = {}, {}, {}
        self.semobj = {}
        for k in self.eng:
            self.sem[k] = es.enter_context(nc.semaphore("e_" + k))
            self.semobj[id(self.sem[k])] = self.sem[k]
            self.cnt[k] = 0
            self.known[k] = {}
        self.free_ds, self.all_ds = [], []
        self.phase_bufs = []
        self.pes = None
        self.uid = 0

    def get_dsem(self):
        if self.free_ds:
            return self.free_ds.pop()
        d = DSem(self.es.enter_context(self.nc.semaphore("d%d" % len(self.all_ds))))
        self.semobj[id(d.sem)] = d.sem
        self.all_ds.append(d)
        return d

    def begin_phase(self):
        self.pes = contextlib.ExitStack()
        self.phase_bufs = []

    def end_phase(self):
        self.barrier()
        for b in self.phase_bufs:
            if b.ds is not None:
                self.free_ds.append(b.ds)
                b.ds = None
        self.pes.close()
        self.pes = None

    def barrier(self):
        toks = {}
        for k in self.eng:
            if self.cnt[k]:
                toks[id(self.sem[k])] = self.cnt[k]
        for d in self.all_ds:
            if d.cnt:
                toks[id(d.sem)] = d.cnt
        for k in self.eng:
            self._wait(k, toks)

    def sb(self, name, shape, dt):
        self.uid += 1
        t = self.pes.enter_context(self.nc.sbuf_tensor("%s_%d" % (name, self.uid), shape, dt))
        b = Buf(self, t, name)
        self.phase_bufs.append(b)
        return b

    def ps(self, name, shape, dt=F32):
        self.uid += 1
        t = self.pes.enter_context(self.nc.psum_tensor("%s_%d" % (name, self.uid), shape, dt))
        b = Buf(self, t, name)
        self.phase_bufs.append(b)
        return b

    def dram(self, name, shape, dt, kind="Internal"):
        t = self.nc.dram_tensor(name, shape, dt, kind=kind).ap()
        return Buf(self, t, name)

    def _merge(self, toks, d):
        for k, v in d.items():
            if toks.get(k, 0) < v:
                toks[k] = v

    def _wait(self, e, toks, skip_self=False):
        kn = self.known[e]
        me = id(self.sem[e])
        for k, v in toks.items():
            if k == me and (skip_self or v > self.cnt[e]):
                continue
            if kn.get(k, 0) >= v:
                continue
            self.eng[e].wait_ge(self.semobj[k], v)
            kn[k] = v

    def _deps(self, reads, writes, par=False):
        toks = {}
        for b in reads:
            self._merge(toks, b.w)
        for b in writes:
            if par:
                self._merge(toks, b.rp)
            else:
                self._merge(toks, b.w)
            self._merge(toks, b.r)
        return toks

    def _commit(self, key, v, reads, writes, acc):
        for b in reads:
            if b.r.get(key, 0) < v:
                b.r[key] = v
        for b in writes:
            if acc:
                if b.w.get(key, 0) < v:
                    b.w[key] = v
            else:
                rp = dict(b.w)
                self._merge(rp, b.r)
                b.rp = rp
                b.w = {key: v}
                b.r = {}

    def op(self, e, fn, reads=(), writes=(), acc=False, par=False):
        acc = acc or par
        self._wait(e, self._deps(reads, writes, par), skip_self=(e == "pe"))
        inst = fn(self.eng[e])
        self.cnt[e] += 1
        inst.then_inc(self.sem[e], 1)
        self._commit(id(self.sem[e]), self.cnt[e], reads, writes, acc)
        return inst

    def dma(self, q, out_ap, in_ap, reads=(), writes=(), sbuf=None, acc=False, par=False):
        acc = acc or par
        self._wait(q, self._deps(reads, writes, par))
        inst = self.eng[q].dma_start(out=out_ap, in_=in_ap)
        d = sbuf.dsem()
        d.cnt += 16
        inst.then_inc(d.sem, 16)
        self._commit(id(d.sem), d.cnt, reads, writes, acc)
        return inst


def t5_bucket_np(rel):
    nb, max_exact = 16, 8
    rel = np.asarray(rel, np.int64)
    ret = np.where(rel > 0, nb, 0)
    n = np.abs(rel)
    nf = np.maximum(n, 1).astype(np.float32)
    large = max_exact + (np.log(nf / np.float32(max_exact)) / np.float32(math.log(1024 / max_exact))
                         * np.float32(nb - max_exact)).astype(np.int32)
    large = np.minimum(large, nb - 1)
    return ret + np.where(n < max_exact, n, large)


def att_mask_tables():
    out = []
    j = np.arange(128)[:, None]
    i = np.arange(128)[None, :]
    for (_w, r) in GROUPS:
        keys, masks = [], []
        for kt in range(2):
            off = j - 64 - i if kt == 0 else j + 64 - i
            band = np.abs(off) <= 64
            bk = t5_bucket_np(off * r)
            for b in range(32):
                m = band & (bk == b)
                if m.any():
                    keys.append((kt, b))
                    masks.append(m.astype(np.float32))
        out.append((keys, np.ascontiguousarray(np.stack(masks, 1))))
    return out


ATT_TABLES = att_mask_tables()


def ret_consts(T):
    NT = T // 128
    j = np.arange(128, dtype=np.float32)[:, None]
    i = np.arange(128, dtype=np.float32)[None, :]
    rc = np.zeros((128, 6 * 128 + 2 + 2 * NT), np.float32)
    rc[:, 0:128] = np.maximum(i - j, 0)
    rc[:, 128:256] = (i >= j)
    rc[:, 256:384] = np.maximum(j - i, 0)
    rc[:, 384:512] = (j > i)
    rc[:, 512:640] = i + 1
    rc[:, 640:768] = 128 - i
    rc[:, 768] = 127 - j[:, 0]
    rc[:, 769] = j[:, 0]
    n = np.arange(NT, dtype=np.float32)[None, :]
    rc[:, 770:770 + NT] = T - 1 - (128 * n + j)
    rc[:, 770 + NT:770 + 2 * NT] = 128 * n + j
    return rc


def misc_consts():
    mc = np.zeros((128, 4 * 128), np.float32)
    mc[:, 0:128] = np.eye(128)
    m = np.arange(128)
    mc[(m + 64) % 128, 128 + m] = 1.0
    mc[:, 256:384] = np.arange(128)[None, :]
    mc[:, 384:400] = np.arange(16)[None, :]
    mc[:, 400:416] = 16.0 * (np.arange(16)[None, :] + 1)
    return mc


def rot_table(pos):
    half = 64
    inv = (1.0 / (10000.0 ** np.linspace(0.0, 1.0, half, dtype=np.float32))).astype(np.float32)
    ang = pos.astype(np.float32)[None, :] * inv[:, None]
    cos = np.cos(ang).astype(np.float32)
    sin = np.sin(ang).astype(np.float32)
    cs = np.empty((128, 2, pos.shape[0]), np.float32)
    cs[:64, 0], cs[64:, 0] = cos, cos
    cs[:64, 1], cs[64:, 1] = -sin, sin
    return cs


def build(T, upto=99, debug=False):
    NT = T // 128
    EXT = T + 2 * HALO
    NG = [T + 128 * r for (_w, r) in GROUPS]
    nc = bass.Bass("TRN2", target_bir_lowering=False)
    es = contextlib.ExitStack()
    with es:
        c = Ctx(nc, es)
        dbg_names = set(debug) if debug else set()

        def inp(name, shape):
            return c.dram(name, shape, F32, kind="ExternalInput")

        xT_ext = inp("xT_ext", [D, EXT])
        xT_nbr = inp("xT_nbr", [D, T])
        x_tm = inp("x_tm", [T, D])
        pT = inp("pT", [256, T])
        flags = inp("flags", [128, 4])
        cs_own = inp("cs_own", [128, 2, T])
        cs_nbr = inp("cs_nbr", [128, 2, T])
        rcst = inp("rcst", [128, 770 + 2 * NT])
        mcst = inp("mcst", [128, 512])
        am_in = [inp("am%d" % g, [128, len(ATT_TABLES[g][0]), 128]) for g in range(3)]
        w_in = inp("w_in", [D, IN_W])
        decay = inp("ret_decay_logit", [2, 8])
        w_ret_o = inp("w_ret_o", [D, D])
        w_att_o = inp("w_att_o", [1024, D])
        w_out = inp("w_out", [D, D])
        rel_bias = inp("rel_bias", [32, 24])
        ln1_g, ln1_b = inp("ln1_g", [1, D]), inp("ln1_b", [1, D])
        ln2_g, ln2_b = inp("ln2_g", [1, D]), inp("ln2_b", [1, D])
        peer_wq = inp("peer_wq", [D, D])
        keysT = inp("keysT", [16, 128, 128])
        uT = inp("uT", [128, KC, 128, 128])
        vj = inp("vj", [128, 128, D])
        w_pe = inp("w_pe", [256, D])
        w_pg = inp("w_pg", [D, D])
        y = c.dram("y", [T, D], F32, kind="ExternalOutput")

        def scr(name, shape, dt=BF16):
            return c.dram(name, shape, dt, kind=("ExternalOutput" if name in dbg_names else "Internal"))

        xb_nat = scr("xb_nat", [KC, 128, EXT])
        xb_r4 = scr("xb_r4", [KC, 128, NG[1]])
        xb_r16 = scr("xb_r16", [KC, 128, NG[2]])
        xb_nbr = scr("xb_nbr", [KC, 128, T])
        pb = scr("pb", [2, 128, T])
        rqT, rkT, rkT_n = scr("rqT", [1024, T]), scr("rkT", [1024, T]), scr("rkT_n", [1024, T])
        rv, rv_n, srg = scr("rv", [T, D]), scr("rv_n", [T, D]), scr("srg", [T, D])
        aqT = [scr("aqT%d" % g, [1024, NG[g]]) for g in range(3)]
        akT = [scr("akT%d" % g, [1024, NG[g]]) for g in range(3)]
        av = [scr("av%d" % g, [NG[g], 1024]) for g in range(3)]
        sgaT, sgbT = scr("sgaT", [D, T]), scr("sgbT", [D, T])
        retT = scr("retT", [D, T])
        att_u = [scr("att_u%d" % g, [T, 8, 132], F32) for g in range(3)]
        attT = scr("attT", [1024, T])
        mergedT = scr("mergedT", [D, T])
        h_tm = scr("h_tm", [T, D], F32)
        hT = scr("hT", [KC, 128, T])
        IT, JT, gT = scr("IT", [128, T], F32), scr("JT", [128, T], F32), scr("gT", [128, T], F32)
        base = scr("base", [T, D], F32)
        uTb = scr("uTb", [128, 128, KC, 128])
        vb = scr("vb", [128, 128, D])

        def phase1():
            c.begin_phase()
            xin = [c.sb("xin%d" % k, [128, EXT], F32) for k in range(2)]
            onat = [c.sb("onat%d" % k, [128, EXT], BF16) for k in range(2)]
            o4 = [c.sb("o4%d" % k, [128, NG[1]], BF16) for k in range(2)]
            o16 = [c.sb("o16%d" % k, [128, NG[2]], BF16) for k in range(2)]
            for dc in range(KC):
                k = dc % 2
                c.dma("sp", xin[k][:], xT_ext[dc * 128:(dc + 1) * 128, :], [xT_ext], [xin[k]], sbuf=xin[k])
                c.op("act", lambda e: e.copy(out=onat[k][:], in_=xin[k][:]), [xin[k]], [onat[k]])
                c.op("dve", lambda e: e.tensor_copy(
                    out=o4[k][:].rearrange("p (z q) -> p z q", z=4),
                    in_=xin[k][:, HALO - 256:HALO - 256 + NG[1]].rearrange("p (q z) -> p z q", z=4)),
                    [xin[k]], [o4[k]])
                c.op("pool", lambda e: e.tensor_copy(
                    out=o16[k][:].rearrange("p (z q) -> p z q", z=16),
                    in_=xin[k][:].rearrange("p (q z) -> p z q", z=16)), [xin[k]], [o16[k]])
                c.dma("pool", xb_nat[dc], onat[k][:], [onat[k]], [xb_nat], sbuf=onat[k], acc=True)
                c.dma("pool", xb_r4[dc], o4[k][:], [o4[k]], [xb_r4], sbuf=o4[k], acc=True)
                c.dma("pool", xb_r16[dc], o16[k][:], [o16[k]], [xb_r16], sbuf=o16[k], acc=True)
            for dc in range(KC):
                k = dc % 2
                c.dma("sp", xin[k][:, 0:T], xT_nbr[dc * 128:(dc + 1) * 128, :], [xT_nbr], [xin[k]], sbuf=xin[k])
                c.op("act", lambda e: e.copy(out=onat[k][:, 0:T], in_=xin[k][:, 0:T]), [xin[k]], [onat[k]])
                c.dma("pool", xb_nbr[dc], onat[k][:, 0:T], [onat[k]], [xb_nbr], sbuf=onat[k], acc=True)
            for dc in range(2):
                k = dc % 2
                c.dma("sp", xin[k][:, 0:T], pT[dc * 128:(dc + 1) * 128, :], [pT], [xin[k]], sbuf=xin[k])
                c.op("act", lambda e: e.copy(out=onat[k][:, 0:T], in_=xin[k][:, 0:T]), [xin[k]], [onat[k]])
                c.dma("pool", pb[dc], onat[k][:, 0:T], [onat[k]], [pb], sbuf=onat[k], acc=True)
            c.end_phase()

        def gemm_phase(jobs):
            c.begin_phase()
            SBK, CB = 2048, 256
            act = c.sb("g_act", [128, KC, SBK], BF16)
            wst = [c.sb("g_wst%d" % k, [128, KC, CB], F32) for k in range(2)]
            wbf = [c.sb("g_wbf%d" % k, [128, KC, CB], BF16) for k in range(2)]
            pss = [c.ps("g_ps%d" % k, [128, 512], F32) for k in range(4)]
            ost = [c.sb("g_ost%d" % k, [128, 512], BF16) for k in range(4)]
            pi = [0]
            wi = [0]

            def load_w(task):
                (wd, wc0, mode, obuf, oc0, func) = task
                k = wi[0] % 2
                wi[0] += 1
                c.dma("sp", wst[k][:], wd[:, wc0:wc0 + CB].rearrange("(kc p) n -> p kc n", p=128), [wd], [wst[k]], sbuf=wst[k])
                c.op("dve", lambda e: e.tensor_copy(out=wbf[k][:, 0:8, :], in_=wst[k][:, 0:8, :]), [wst[k]], [wbf[k]])
                c.op("pool", lambda e: e.tensor_copy(out=wbf[k][:, 8:16, :], in_=wst[k][:, 8:16, :]), [wst[k]], [wbf[k]], par=True)
                return k

            def compute(task, k, s0, sn):
                (wd, wc0, mode, obuf, oc0, func) = task
                if mode == "fm":
                    for cc in range(CB // 128):
                        for t0 in range(0, sn, 512):
                            tn = min(512, sn - t0)
                            p = pi[0] % 4
                            pi[0] += 1
                            for kc in range(KC):
                                c.op("pe", lambda e: e.matmul(pss[p][:, 0:tn], lhsT=wbf[k][:, kc, cc * 128:(cc + 1) * 128],
                                                            rhs=act[:, kc, t0:t0 + tn], start=(kc == 0), stop=(kc == KC - 1)),
                                     [wbf[k], act], [pss[p]])
                            c.op("act", lambda e: e.activation(out=ost[p][:, 0:tn], in_=pss[p][:, 0:tn], func=func), [pss[p]], [ost[p]])
                            r0 = oc0 + cc * 128
                            c.dma("sp", obuf[r0:r0 + 128, s0 + t0:s0 + t0 + tn], ost[p][:, 0:tn], [ost[p]], [obuf], sbuf=ost[p], par=True)
                else:
                    for t0 in range(0, sn, 128):
                        p = pi[0] % 4
                        pi[0] += 1
                        for kc in range(KC):
                            c.op("pe", lambda e: e.matmul(pss[p][:, 0:CB], lhsT=act[:, kc, t0:t0 + 128],
                                                        rhs=wbf[k][:, kc, :], start=(kc == 0), stop=(kc == KC - 1)),
                                 [wbf[k], act], [pss[p]])
                        c.op("act", lambda e: e.activation(out=ost[p][:, 0:CB], in_=pss[p][:, 0:CB], func=func), [pss[p]], [ost[p]])
                        c.dma("sp", obuf[s0 + t0:s0 + t0 + 128, oc0:oc0 + CB], ost[p][:, 0:CB], [ost[p]], [obuf], sbuf=ost[p], par=True)

            for (abuf, tok0, ntok, cols) in jobs:
                tasks = []
                for (wd, col0, ncols, mode, obuf, oc0, func) in cols:
                    for cb0 in range(0, ncols, CB):
                        tasks.append((wd, col0 + cb0, mode, obuf, oc0 + cb0, func))
                for s0 in range(0, ntok, SBK):
                    sn = min(SBK, ntok - s0)
                    for kc in range(KC):
                        c.dma("sp", act[:, kc, 0:sn], abuf[kc][:, tok0 + s0:tok0 + s0 + sn], [abuf], [act],
                              sbuf=act, par=(kc > 0))
                    knext = load_w(tasks[0])
                    for i, task in enumerate(tasks):
                        k = knext
                        if i + 1 < len(tasks):
                            knext = load_w(tasks[i + 1])
                        compute(task, k, s0, sn)
            c.end_phase()

        def phaseA():
            CP = AF.Copy
            jobs = [
                (xb_nat, HALO, T, [(w_in, 0, 1024, "fm", rqT, 0, CP), (w_in, 1024, 1024, "fm", rkT, 0, CP),
                                   (w_in, 2048, 2048, "tm", rv, 0, CP), (w_in, 4096, 2048, "tm", srg, 0, AF.Silu),
                                   (w_in, 15360, 2048, "fm", sgaT, 0, AF.Sigmoid), (w_in, 17408, 2048, "fm", sgbT, 0, AF.Sigmoid)]),
                (xb_nbr, 0, T, [(w_in, 1024, 1024, "fm", rkT_n, 0, CP), (w_in, 2048, 2048, "tm", rv_n, 0, CP)]),
            ]
            gx = [(xb_nat, HALO - 64), (xb_r4, 0), (xb_r16, 0)]
            for g in range(3):
                jobs.append((gx[g][0], gx[g][1], NG[g],
                             [(w_in, 6144 + g * 1024, 1024, "fm", aqT[g], 0, CP), (w_in, 9216 + g * 1024, 1024, "fm", akT[g], 0, CP),
                              (w_in, 12288 + g * 1024, 1024, "tm", av[g], 0, CP)]))
            gemm_phase(jobs)

        def phase2():
            c.begin_phase()
            rc = c.sb("rc", [128, 770 + 2 * NT], F32)
            mc = c.sb("mc", [128, 512], F32)
            fl = c.sb("fl", [128, 4], F32)
            lg = c.sb("lg", [128, 16], F32)
            c.dma("sp", rc[:], rcst[:], [rcst], [rc], sbuf=rc)
            c.dma("sp", mc[:], mcst[:], [mcst], [mc], sbuf=mc)
            c.dma("sp", fl[:], flags[:], [flags], [fl], sbuf=fl)
            c.dma("sp", lg[:], decay[:].rearrange("a b -> (a b)").partition_broadcast(128), [decay], [lg], sbuf=lg)
            c.op("act", lambda e: e.activation(out=lg[:], in_=lg[:], func=AF.Exp, scale=-1.0), [lg], [lg])
            c.op("dve", lambda e: e.tensor_scalar(out=lg[:], in0=lg[:], scalar1=1.0, scalar2=None, op0=ALU.add), [lg], [lg])
            c.op("act", lambda e: e.activation(out=lg[:], in_=lg[:], func=AF.Ln), [lg], [lg])
            c.op("dve", lambda e: e.tensor_scalar(out=lg[:], in0=lg[:], scalar1=-1.0, scalar2=None, op0=ALU.mult), [lg], [lg])
            identb = c.sb("identb", [128, 128], BF16)
            permb = c.sb("permb", [128, 128], BF16)
            c.op("dve", lambda e: e.tensor_copy(out=identb[:], in_=mc[:, 0:128]), [mc], [identb])
            c.op("dve", lambda e: e.tensor_copy(out=permb[:], in_=mc[:, 128:256]), [mc], [permb])
            qR, kR, kRn = c.sb("qR", [128, T], BF16), c.sb("kR", [128, T], BF16), c.sb("kRn", [128, T], BF16)
            raw = [c.sb("raw%d" % k, [128, 512], BF16) for k in range(2)]
            cst = [c.sb("cst%d" % k, [128, 2, 512], F32) for k in range(2)]
            tmp1 = [c.sb("tmp1%d" % k, [128, 512], F32) for k in range(2)]
            tmp2 = [c.sb("tmp2%d" % k, [128, 512], F32) for k in range(2)]
            vt = c.sb("vt", [128, NT, 256], BF16)
            vtn = c.sb("vtn", [128, NT, 256], BF16)
            gt = c.sb("gt", [128, NT, 256], BF16)
            Rfs = c.sb("Rfs", [128, NT, 256], BF16)
            Rbs = c.sb("Rbs", [128, NT, 256], BF16)
            kbs = c.sb("kbs", [128, NT, 128], BF16)
            retTs = c.sb("retTs", [128, 2, T], BF16)
            DT = c.sb("DT", [128, 128], F32)
            dtmp = c.sb("dtmp", [128, 128], F32)
            xif, xib = c.sb("xif", [128, 128], F32), c.sb("xib", [128, 128], F32)
            hv = c.sb("hv", [128, 8 + 2 * NT], F32)
            Rf, Rb = c.sb("Rf", [128, 256], F32), c.sb("Rb", [128, 256], F32)
            kf = [c.sb("kf%d" % k, [128, 128], BF16) for k in range(2)]
            kb2 = [c.sb("kb2%d" % k, [128, 128], BF16) for k in range(2)]
            STs = [c.sb("STs%d" % k, [128, 128], BF16) for k in range(2)]
            qfs = [c.sb("qfs%d" % k, [128, 128], BF16) for k in range(2)]
            qbs = [c.sb("qbs%d" % k, [128, 128], BF16) for k in range(2)]
            st6 = [c.sb("st6%d" % k, [128, 8], F32) for k in range(2)]
            mv = [c.sb("mv%d" % k, [128, 4], F32) for k in range(2)]
            on = [c.sb("on%d" % k, [128, 256], F32) for k in range(2)]
            rt = [c.sb("rt%d" % k, [128, 256], BF16) for k in range(2)]
            ps_r = [c.ps("ps_r%d" % k, [128, 512], F32) for k in range(2)]
            ps_t = [c.ps("ps_t%d" % k, [128, 256], BF16) for k in range(2)]
            ps_k = [c.ps("ps_k%d" % k, [128, 256], F32) for k in range(2)]
            ps_s = c.ps("ps_s", [128, 128], F32)
            ps_o = c.ps("ps_o", [128, 256], F32)
            SC = 128.0 ** -0.5
            blk = [0]

            def rotary(src, cs, dst, h):
                for t0 in range(0, T, 512):
                    k = blk[0] % 2
                    blk[0] += 1
                    c.dma("sp", raw[k][:], src[h * 128:(h + 1) * 128, t0:t0 + 512], [src], [raw[k]], sbuf=raw[k])
                    c.dma("sp", cst[k][:], cs[:, :, t0:t0 + 512], [cs], [cst[k]], sbuf=cst[k])
                    c.op("pe", lambda e: e.matmul(ps_r[k][:], lhsT=permb[:], rhs=raw[k][:], start=True, stop=True), [permb, raw[k]], [ps_r[k]])
                    c.op("pool", lambda e: e.tensor_tensor(out=tmp1[k][:], in0=raw[k][:], in1=cst[k][:, 0, :], op=ALU.mult), [raw[k], cst[k]], [tmp1[k]])
                    c.op("dve", lambda e: e.tensor_tensor(out=tmp2[k][:], in0=ps_r[k][:], in1=cst[k][:, 1, :], op=ALU.mult), [ps_r[k], cst[k]], [tmp2[k]])
                    c.op("dve", lambda e: e.tensor_tensor(out=dst[:, t0:t0 + 512], in0=tmp1[k][:], in1=tmp2[k][:], op=ALU.add), [tmp1[k], tmp2[k]], [dst], acc=True)

            for h in range(8):
                lf, lb = lg[:, h:h + 1], lg[:, 8 + h:9 + h]
                c.op("act", lambda e: e.activation(out=DT[:], in_=rc[:, 0:128], func=AF.Exp, scale=lf), [rc, lg], [DT])
                c.op("dve", lambda e: e.tensor_tensor(out=DT[:], in0=DT[:], in1=rc[:, 128:256], op=ALU.mult), [DT, rc], [DT])
                c.op("act", lambda e: e.activation(out=dtmp[:], in_=rc[:, 256:384], func=AF.Exp, scale=lb), [rc, lg], [dtmp])
                c.op("dve", lambda e: e.tensor_tensor(out=dtmp[:], in0=dtmp[:], in1=rc[:, 384:512], op=ALU.mult), [dtmp, rc], [dtmp])
                c.op("dve", lambda e: e.scalar_tensor_tensor(out=DT[:], in0=DT[:], scalar=1.0, in1=dtmp[:], op0=ALU.mult, op1=ALU.add), [DT, dtmp], [DT])
                c.op("dve", lambda e: e.tensor_scalar(out=DT[:], in0=DT[:], scalar1=SC, scalar2=None, op0=ALU.mult), [DT], [DT])
                c.op("act", lambda e: e.activation(out=xif[:], in_=rc[:, 512:640], func=AF.Exp, scale=lf), [rc, lg], [xif])
                c.op("act", lambda e: e.activation(out=xib[:], in_=rc[:, 640:768], func=AF.Exp, scale=lb), [rc, lg], [xib])
                c.op("act", lambda e: e.activation(out=hv[:, 0:1], in_=rc[:, 768:769], func=AF.Exp, scale=lf), [rc, lg], [hv])
                c.op("act", lambda e: e.activation(out=hv[:, 1:2], in_=rc[:, 769:770], func=AF.Exp, scale=lb), [rc, lg], [hv])
                c.op("act", lambda e: e.activation(out=hv[:, 8:8 + NT], in_=rc[:, 770:770 + NT], func=AF.Exp, scale=lf), [rc, lg], [hv])
                c.op("act", lambda e: e.activation(out=hv[:, 8 + NT:8 + 2 * NT], in_=rc[:, 770 + NT:770 + 2 * NT], func=AF.Exp, scale=lb), [rc, lg], [hv])
                c.op("dve", lambda e: e.tensor_scalar(out=hv[:, 0:2], in0=hv[:, 0:2], scalar1=SC, scalar2=None, op0=ALU.mult), [hv], [hv])
                c.op("dve", lambda e: e.tensor_scalar(out=hv[:, 8:8 + 2 * NT], in0=hv[:, 8:8 + 2 * NT], scalar1=SC, scalar2=None, op0=ALU.mult), [hv], [hv])
                c.op("dve", lambda e: e.tensor_scalar(out=hv[:, 2:3], in0=lg[:, h:h + 1], scalar1=128.0, scalar2=None, op0=ALU.mult), [lg], [hv])
                c.op("dve", lambda e: e.tensor_scalar(out=hv[:, 3:4], in0=lg[:, 8 + h:9 + h], scalar1=128.0, scalar2=None, op0=ALU.mult), [lg], [hv])
                c.op("act", lambda e: e.activation(out=hv[:, 2:4], in_=hv[:, 2:4], func=AF.Exp), [hv], [hv])
                rotary(rqT, cs_own, qR, h)
                rotary(rkT, cs_own, kR, h)
                rotary(rkT_n, cs_nbr, kRn, h)
                c.dma("sp", vt[:], rv[:, h * 256:(h + 1) * 256].rearrange("(n p) c -> p n c", p=128), [rv], [vt], sbuf=vt)
                c.dma("sp", vtn[:], rv_n[:, h * 256:(h + 1) * 256].rearrange("(n p) c -> p n c", p=128), [rv_n], [vtn], sbuf=vtn)
                c.dma("sp", gt[:], srg[:, h * 256:(h + 1) * 256].rearrange("(n p) c -> p n c", p=128), [srg], [gt], sbuf=gt)
                for n in range(NT):
                    k = n % 2
                    c.op("pe", lambda e: e.transpose(out=ps_t[k][:, 0:128], in_=kRn[:, n * 128:(n + 1) * 128], identity=identb[:]), [kRn, identb], [ps_t[k]])
                    c.op("dve", lambda e: e.tensor_scalar(out=kf[k][:], in0=ps_t[k][:, 0:128], scalar1=hv[:, 8 + n:9 + n], scalar2=None, op0=ALU.mult), [ps_t[k], hv], [kf[k]])
                    c.op("act", lambda e: e.activation(out=kb2[k][:], in_=ps_t[k][:, 0:128], func=AF.Copy, scale=hv[:, 8 + NT + n:9 + NT + n]), [ps_t[k], hv], [kb2[k]])
                    c.op("pe", lambda e: e.matmul(ps_k[0][:], lhsT=kf[k][:], rhs=vtn[:, n, :], start=(n == 0), stop=(n == NT - 1)), [kf[k], vtn], [ps_k[0]])
                    c.op("pe", lambda e: e.matmul(ps_k[1][:], lhsT=kb2[k][:], rhs=vtn[:, n, :], start=(n == 0), stop=(n == NT - 1)), [kb2[k], vtn], [ps_k[1]])
                c.op("dve", lambda e: e.tensor_scalar(out=Rf[:], in0=ps_k[0][:], scalar1=fl[:, 0:1], scalar2=None, op0=ALU.mult), [ps_k[0], fl], [Rf])
                c.op("dve", lambda e: e.tensor_scalar(out=Rb[:], in0=ps_k[1][:], scalar1=fl[:, 1:2], scalar2=None, op0=ALU.mult), [ps_k[1], fl], [Rb])
                for n in range(NT):
                    k = n % 2
                    c.op("act", lambda e: e.copy(out=Rfs[:, n, :], in_=Rf[:]), [Rf], [Rfs], acc=True)
                    c.op("pe", lambda e: e.transpose(out=ps_t[k][:, 0:128], in_=kR[:, n * 128:(n + 1) * 128], identity=identb[:]), [kR, identb], [ps_t[k]])
                    c.op("dve", lambda e: e.tensor_scalar(out=kf[k][:], in0=ps_t[k][:, 0:128], scalar1=hv[:, 0:1], scalar2=None, op0=ALU.mult), [ps_t[k], hv], [kf[k]])
                    c.op("act", lambda e: e.activation(out=kbs[:, n, :], in_=ps_t[k][:, 0:128], func=AF.Copy, scale=hv[:, 1:2]), [ps_t[k], hv], [kbs], acc=True)
                    c.op("pe", lambda e: e.matmul(ps_k[k][:], lhsT=kf[k][:], rhs=vt[:, n, :], start=True, stop=True), [kf[k], vt], [ps_k[k]])
                    c.op("dve", lambda e: e.scalar_tensor_tensor(out=Rf[:], in0=Rf[:], scalar=hv[:, 2:3], in1=ps_k[k][:], op0=ALU.mult, op1=ALU.add), [Rf, hv, ps_k[k]], [Rf])
                for n in range(NT - 1, -1, -1):
                    k = n % 2
                    c.op("act", lambda e: e.copy(out=Rbs[:, n, :], in_=Rb[:]), [Rb], [Rbs], acc=True)
                    c.op("pe", lambda e: e.matmul(ps_k[k][:], lhsT=kbs[:, n, :], rhs=vt[:, n, :], start=True, stop=True), [kbs, vt], [ps_k[k]])
                    c.op("dve", lambda e: e.scalar_tensor_tensor(out=Rb[:], in0=Rb[:], scalar=hv[:, 3:4], in1=ps_k[k][:], op0=ALU.mult, op1=ALU.add), [Rb, hv, ps_k[k]], [Rb])
                def stA(n):
                    k = n % 2
                    sl = slice(n * 128, (n + 1) * 128)
                    c.op("pe", lambda e: e.matmul(ps_s[:], lhsT=kR[:, sl], rhs=qR[:, sl], start=True, stop=True), [kR, qR], [ps_s])
                    c.op("dve", lambda e: e.tensor_tensor(out=STs[k][:], in0=ps_s[:], in1=DT[:], op=ALU.mult), [ps_s, DT], [STs[k]])
                    c.op("pool", lambda e: e.tensor_tensor(out=qfs[k][:], in0=qR[:, sl], in1=xif[:], op=ALU.mult), [qR, xif], [qfs[k]])
                    c.op("pool", lambda e: e.tensor_tensor(out=qbs[k][:], in0=qR[:, sl], in1=xib[:], op=ALU.mult), [qR, xib], [qbs[k]])

                def stB(n):
                    k = n % 2
                    c.op("pe", lambda e: e.matmul(ps_o[:], lhsT=STs[k][:], rhs=vt[:, n, :], start=True, stop=False), [STs[k], vt], [ps_o])
                    c.op("pe", lambda e: e.matmul(ps_o[:], lhsT=qfs[k][:], rhs=Rfs[:, n, :], start=False, stop=False), [qfs[k], Rfs], [ps_o])
                    c.op("pe", lambda e: e.matmul(ps_o[:], lhsT=qbs[k][:], rhs=Rbs[:, n, :], start=False, stop=True), [qbs[k], Rbs], [ps_o])
                    c.op("dve", lambda e: e.bn_stats(out=st6[k][:, 0:6], in_=ps_o[:]), [ps_o], [st6[k]])
                    c.op("dve", lambda e: e.bn_aggr(out=mv[k][:, 0:2], in_=st6[k][:, 0:6]), [st6[k]], [mv[k]])
                    c.op("dve", lambda e: e.tensor_scalar(out=mv[k][:, 2:3], in0=mv[k][:, 1:2], scalar1=LN_EPS, scalar2=None, op0=ALU.add), [mv[k]], [mv[k]])
                    c.op("act", lambda e: e.sqrt(out=mv[k][:, 2:3], in_=mv[k][:, 2:3]), [mv[k]], [mv[k]])
                    c.op("dve", lambda e: e.reciprocal(out=mv[k][:, 3:4], in_=mv[k][:, 2:3]), [mv[k]], [mv[k]])
                    c.op("dve", lambda e: e.tensor_scalar(out=on[k][:], in0=ps_o[:], scalar1=mv[k][:, 0:1], scalar2=mv[k][:, 3:4], op0=ALU.subtract, op1=ALU.mult), [ps_o, mv[k]], [on[k]])
                    c.op("pool", lambda e: e.tensor_tensor(out=rt[k][:], in0=on[k][:], in1=gt[:, n, :], op=ALU.mult), [on[k], gt], [rt[k]])

                def stC(n):
                    k = n % 2
                    sl = slice(n * 128, (n + 1) * 128)
                    for hf in range(2):
                        c.op("pe", lambda e: e.transpose(out=ps_t[k][:, hf * 128:(hf + 1) * 128], in_=rt[k][:, hf * 128:(hf + 1) * 128], identity=identb[:]), [rt[k], identb], [ps_t[k]], acc=(hf == 1))
                    c.op("act", lambda e: e.copy(out=retTs[:, :, sl], in_=ps_t[k][:].rearrange("p (a b) -> p a b", a=2)), [ps_t[k]], [retTs], acc=True)

                stA(0)
                for n in range(NT):
                    if n + 1 < NT:
                        stA(n + 1)
                    stB(n)
                    if n >= 1:
                        stC(n - 1)
                stC(NT - 1)
                c.dma("pool", retT[h * 256:(h + 1) * 256, :].rearrange("(a p) t -> p a t", p=128), retTs[:], [retTs], [retT], sbuf=retTs, acc=True)
            c.end_phase()


        def phase3():
            c.begin_phase()
            NTVm = NG[2] // 128
            fl = c.sb("fl", [128, 4], F32)
            eb = c.sb("eb", [128, 768], F32)
            mc = c.sb("mc", [128, 512], F32)
            c.dma("sp", fl[:], flags[:], [flags], [fl], sbuf=fl)
            c.dma("sp", mc[:], mcst[:], [mcst], [mc], sbuf=mc)
            c.dma("sp", eb[:], rel_bias[:].rearrange("a b -> (a b)").partition_broadcast(128), [rel_bias], [eb], sbuf=eb)
            c.op("act", lambda e: e.activation(out=eb[:], in_=eb[:], func=AF.Exp), [eb], [eb])
            identb = c.sb("identb", [128, 128], BF16)
            c.op("dve", lambda e: e.tensor_copy(out=identb[:], in_=mc[:, 0:128]), [mc], [identb])
            nmax = max(len(t[0]) for t in ATT_TABLES)
            am = c.sb("am", [128, nmax, 128], F32)
            E = c.sb("E", [128, 2, 128], F32)
            QT = c.sb("QT", [128, NG[2]], BF16)
            KT = c.sb("KT", [128, NG[2]], BF16)
            Va = c.sb("Va", [128, NTVm, 132], BF16)
            ou = c.sb("ou", [128, NT, 132], F32)
            c.op("pool", lambda e: e.memset(ou[:], 0.0), [], [ou])
            Pe = [c.sb("Pe%d" % k, [128, 128], F32) for k in range(4)]
            Pm = [c.sb("Pm%d" % k, [128, 128], BF16) for k in range(4)]
            ps_s = [c.ps("ps_s%d" % k, [128, 128], F32) for k in range(4)]
            ps_o = [c.ps("ps_o%d" % k, [128, 132], F32) for k in range(2)]
            ps_t = [c.ps("ps_t%d" % k, [128, 1024], BF16) for k in range(2)]
            SC = 128.0 ** -0.5
            it = 0
            for g, (_w, r) in enumerate(GROUPS):
                keys = ATT_TABLES[g][0]
                PZ = T // r + 128
                TPZ = PZ // 128
                CQ = T // r // 128
                NTV = NG[g] // 128
                c.dma("sp", am[:, 0:len(keys), :], am_in[g][:], [am_in[g]], [am], sbuf=am)
                for h in range(8):
                    gh = g * 8 + h
                    seen = set()
                    for idx, (kt, b) in enumerate(keys):
                        col = eb[:, b * 24 + gh:b * 24 + gh + 1]
                        if kt not in seen:
                            seen.add(kt)
                            c.op("dve", lambda e: e.tensor_scalar(out=E[:, kt, :], in0=am[:, idx, :], scalar1=col, scalar2=None, op0=ALU.mult), [am, eb], [E], acc=(kt == 1))
                        else:
                            c.op("dve", lambda e: e.scalar_tensor_tensor(out=E[:, kt, :], in0=am[:, idx, :], scalar=col, in1=E[:, kt, :], op0=ALU.mult, op1=ALU.add), [am, eb, E], [E], acc=True)
                    c.dma("sp", QT[:, 0:NG[g]], aqT[g][h * 128:(h + 1) * 128, :], [aqT[g]], [QT], sbuf=QT)
                    c.dma("sp", KT[:, 0:NG[g]], akT[g][h * 128:(h + 1) * 128, :], [akT[g]], [KT], sbuf=KT)
                    c.dma("sp", Va[:, 0:NTV, 0:128], av[g][:, h * 128:(h + 1) * 128].rearrange("(n p) c -> p n c", p=128), [av[g]], [Va], sbuf=Va)
                    c.op("pool", lambda e: e.memset(Va[:, 0:NTV, 128:129], 1.0), [], [Va], acc=True)
                    c.op("dve", lambda e: e.tensor_scalar(out=Va[:, 0:NTV:TPZ, 0:129], in0=Va[:, 0:NTV:TPZ, 0:129], scalar1=fl[:, 2:3], scalar2=None, op0=ALU.mult), [Va, fl], [Va], acc=True)
                    c.op("dve", lambda e: e.tensor_scalar(out=Va[:, TPZ - 1:NTV:TPZ, 0:129], in0=Va[:, TPZ - 1:NTV:TPZ, 0:129], scalar1=fl[:, 3:4], scalar2=None, op0=ALU.mult), [Va, fl], [Va], acc=True)
                    tiles = [(z, cq) for z in range(r) for cq in range(CQ)]

                    def emit_S(i):
                        z, cq = tiles[i]
                        qcol = z * PZ + 128 * cq + 64
                        for kt in range(2):
                            kcol = z * PZ + 128 * (cq + kt)
                            pS = ps_s[(i % 2) * 2 + kt]
                            c.op("pe", lambda e: e.matmul(pS[:], lhsT=KT[:, kcol:kcol + 128], rhs=QT[:, qcol:qcol + 128], start=True, stop=True), [KT, QT], [pS])

                    def emit_rest(i):
                        z, cq = tiles[i]
                        po = ps_o[i % 2]
                        for kt in range(2):
                            m = cq + kt
                            pS = ps_s[(i % 2) * 2 + kt]
                            pe_ = Pe[(i % 2) * 2 + kt]
                            pk = Pm[(i % 2) * 2 + kt]
                            c.op("act", lambda e: e.activation(out=pe_[:], in_=pS[:], func=AF.Exp, scale=SC), [pS], [pe_])
                            c.op("dve", lambda e: e.tensor_tensor(out=pk[:], in0=pe_[:], in1=E[:, kt, :], op=ALU.mult), [pe_, E], [pk])
                            c.op("pe", lambda e: e.matmul(po[:, 0:129], lhsT=pk[:], rhs=Va[:, z * TPZ + m, 0:129], start=(kt == 0), stop=(kt == 1)), [pk, Va], [po])
                        c.op("act", lambda e: e.copy(out=ou[:, z * CQ + cq, 0:129], in_=po[:, 0:129]), [po], [ou], acc=True)

                    emit_S(0)
                    for i in range(len(tiles)):
                        if i + 1 < len(tiles):
                            emit_S(i + 1)
                        emit_rest(i)
                    c.dma("pool", att_u[g][:].rearrange("(c i z) h w -> i z c h w", i=128, z=r)[:, :, :, h, :],
                          ou[:].rearrange("p (z c) w -> p z c w", z=r), [ou], [att_u[g]], sbuf=ou, acc=True)
            au = [c.sb("au%d" % k, [128, 3, 8, 132], F32) for k in range(2)]
            rden = [c.sb("rden%d" % k, [128, 8], F32) for k in range(2)]
            ab = [c.sb("ab%d" % k, [128, 8, 128], BF16) for k in range(2)]
            ats = [c.sb("ats%d" % k, [128, 8, 128], BF16) for k in range(2)]
            for n in range(NT):
                k = n % 2
                for g in range(3):
                    c.dma("sp", au[k][:, g], att_u[g][n * 128:(n + 1) * 128], [att_u[g]], [au[k]], sbuf=au[k], acc=(g > 0))
                c.op("dve", lambda e: e.tensor_tensor(out=au[k][:, 0], in0=au[k][:, 0], in1=au[k][:, 1], op=ALU.add), [au[k]], [au[k]])
                c.op("dve", lambda e: e.tensor_tensor(out=au[k][:, 0], in0=au[k][:, 0], in1=au[k][:, 2], op=ALU.add), [au[k]], [au[k]])
                c.op("dve", lambda e: e.reciprocal(out=rden[k][:], in_=au[k][:, 0, :, 128]), [au[k]], [rden[k]])
                c.op("dve", lambda e: e.tensor_tensor(out=ab[k][:], in0=au[k][:, 0, :, 0:128], in1=rden[k][:].unsqueeze(2).to_broadcast([128, 8, 128]), op=ALU.mult), [au[k], rden[k]], [ab[k]])
                for hh in range(8):
                    c.op("pe", lambda e: e.transpose(out=ps_t[k][:, hh * 128:(hh + 1) * 128], in_=ab[k][:, hh, :], identity=identb[:]), [ab[k], identb], [ps_t[k]], acc=(hh > 0))
                c.op("act", lambda e: e.copy(out=ats[k][:], in_=ps_t[k][:].rearrange("p (a b) -> p a b", a=8)), [ps_t[k]], [ats[k]])
                c.dma("pool", attT[:, n * 128:(n + 1) * 128].rearrange("(a p) t -> p a t", p=128), ats[k][:], [ats[k]], [attT], sbuf=ats[k], acc=True)
            c.end_phase()

        def load_weight_bf16(wd, nk, dst, wst, ncols=D, q0=0):
            for i, c0 in enumerate(range(0, ncols, 256)):
                k = i % 2
                c.dma("sp", wst[k][:, 0:nk, :], wd[:, c0:c0 + 256].rearrange("(kc p) n -> p kc n", p=128), [wd], [wst[k]], sbuf=wst[k])
                eng = "dve" if k == 0 else "pool"
                c.op(eng, lambda e: e.tensor_copy(out=dst[:, 0:nk, q0 + c0:q0 + c0 + 256], in_=wst[k][:, 0:nk, :]), [wst[k]], [dst], acc=True)

        def layernorm(src, dst, gam, bet, st, mv):
            for q in range(4):
                c.op("dve", lambda e: e.bn_stats(out=st[:, q * 6:(q + 1) * 6], in_=src[:, q * 512:(q + 1) * 512]), [src], [st], acc=(q > 0))
            c.op("dve", lambda e: e.bn_aggr(out=mv[:, 0:2], in_=st[:, 0:24]), [st], [mv])
            c.op("dve", lambda e: e.tensor_scalar(out=mv[:, 2:3], in0=mv[:, 1:2], scalar1=LN_EPS, scalar2=None, op0=ALU.add), [mv], [mv])
            c.op("act", lambda e: e.sqrt(out=mv[:, 2:3], in_=mv[:, 2:3]), [mv], [mv])
            c.op("dve", lambda e: e.reciprocal(out=mv[:, 3:4], in_=mv[:, 2:3]), [mv], [mv])
            c.op("dve", lambda e: e.tensor_scalar(out=dst[:], in0=src[:], scalar1=mv[:, 0:1], scalar2=mv[:, 3:4], op0=ALU.subtract, op1=ALU.mult), [src, mv], [dst])
            c.op("pool", lambda e: e.tensor_tensor(out=dst[:], in0=dst[:], in1=gam[:], op=ALU.mult), [dst, gam], [dst])
            c.op("pool", lambda e: e.tensor_tensor(out=dst[:], in0=dst[:], in1=bet[:], op=ALU.add), [dst, bet], [dst])

        def phase4a():
            c.begin_phase()
            wst = [c.sb("wst%d" % k, [128, KC, 256], F32) for k in range(2)]
            Wro = c.sb("Wro", [128, KC, D], BF16)
            Wao = c.sb("Wao", [128, 8, D], BF16)
            load_weight_bf16(w_ret_o, KC, Wro, wst)
            load_weight_bf16(w_att_o, 8, Wao, wst)
            rT = [c.sb("rT%d" % k, [128, KC, 512], BF16) for k in range(2)]
            aT = [c.sb("aT%d" % k, [128, 8, 512], BF16) for k in range(2)]
            ga = [c.sb("ga%d" % k, [128, 512], BF16) for k in range(2)]
            gb = [c.sb("gb%d" % k, [128, 512], BF16) for k in range(2)]
            m1 = [c.sb("m1%d" % k, [128, 512], F32) for k in range(2)]
            m2 = [c.sb("m2%d" % k, [128, 512], F32) for k in range(2)]
            mo = [c.sb("mo%d" % k, [128, 512], BF16) for k in range(2)]
            ps1 = [c.ps("ps1%d" % k, [128, 512], F32) for k in range(2)]
            ps2 = [c.ps("ps2%d" % k, [128, 512], F32) for k in range(2)]
            it = 0
            for tb in range(T // 512):
                kb = tb % 2
                ts_ = slice(tb * 512, (tb + 1) * 512)
                c.dma("sp", rT[kb][:], retT[:, ts_].rearrange("(kc p) t -> p kc t", p=128), [retT], [rT[kb]], sbuf=rT[kb])
                c.dma("sp", aT[kb][:], attT[:, ts_].rearrange("(kc p) t -> p kc t", p=128), [attT], [aT[kb]], sbuf=aT[kb])
                for cc in range(KC):
                    k = it % 2
                    it += 1
                    cs_ = slice(cc * 128, (cc + 1) * 128)
                    c.dma("sp", ga[k][:], sgaT[cs_, ts_], [sgaT], [ga[k]], sbuf=ga[k])
                    c.dma("sp", gb[k][:], sgbT[cs_, ts_], [sgbT], [gb[k]], sbuf=gb[k])
                    for kc in range(KC):
                        c.op("pe", lambda e: e.matmul(ps1[k][:], lhsT=Wro[:, kc, cs_], rhs=rT[kb][:, kc, :], start=(kc == 0), stop=(kc == KC - 1)), [Wro, rT[kb]], [ps1[k]])
                    for kc in range(8):
                        c.op("pe", lambda e: e.matmul(ps2[k][:], lhsT=Wao[:, kc, cs_], rhs=aT[kb][:, kc, :], start=(kc == 0), stop=(kc == 7)), [Wao, aT[kb]], [ps2[k]])
                    c.op("dve", lambda e: e.tensor_tensor(out=m1[k][:], in0=ps1[k][:], in1=ga[k][:], op=ALU.mult), [ps1[k], ga[k]], [m1[k]])
                    c.op("dve", lambda e: e.tensor_tensor(out=m2[k][:], in0=ps2[k][:], in1=gb[k][:], op=ALU.mult), [ps2[k], gb[k]], [m2[k]])
                    c.op("pool", lambda e: e.tensor_tensor(out=mo[k][:], in0=m1[k][:], in1=m2[k][:], op=ALU.add), [m1[k], m2[k]], [mo[k]])
                    c.dma("pool", mergedT[cs_, ts_], mo[k][:], [mo[k]], [mergedT], sbuf=mo[k], acc=True)
            c.end_phase()

        def phase4b():
            c.begin_phase()
            wst = [c.sb("wst%d" % k, [128, KC, 256], F32) for k in range(2)]
            Wo = c.sb("Wo", [128, KC, D], BF16)
            load_weight_bf16(w_out, KC, Wo, wst)
            mc = c.sb("mc", [128, 512], F32)
            c.dma("sp", mc[:], mcst[:], [mcst], [mc], sbuf=mc)
            identb = c.sb("identb", [128, 128], BF16)
            c.op("dve", lambda e: e.tensor_copy(out=identb[:], in_=mc[:, 0:128]), [mc], [identb])
            gam, bet = c.sb("gam", [128, D], F32), c.sb("bet", [128, D], F32)
            c.dma("sp", gam[:], ln1_g[:].rearrange("a b -> (a b)").partition_broadcast(128), [ln1_g], [gam], sbuf=gam)
            c.dma("sp", bet[:], ln1_b[:].rearrange("a b -> (a b)").partition_broadcast(128), [ln1_b], [bet], sbuf=bet)
            mt = [c.sb("mt%d" % k, [128, KC, 128], BF16) for k in range(2)]
            xt = [c.sb("xt%d" % k, [128, D], F32) for k in range(2)]
            y1 = [c.sb("y1%d" % k, [128, D], F32) for k in range(2)]
            hh = [c.sb("hh%d" % k, [128, D], F32) for k in range(2)]
            hb = [c.sb("hb%d" % k, [128, D], BF16) for k in range(2)]
            hTs = [c.sb("hTs%d" % k, [128, KC, 128], BF16) for k in range(2)]
            st = [c.sb("st%d" % k, [128, 24], F32) for k in range(2)]
            mv = [c.sb("mv%d" % k, [128, 4], F32) for k in range(2)]
            pss = [c.ps("pss%d" % k, [128, 512], F32) for k in range(4)]
            ps_t = [c.ps("ps_t%d" % k, [128, 1024], BF16) for k in range(2)]
            for n in range(NT):
                k = n % 2
                sl = slice(n * 128, (n + 1) * 128)
                c.dma("sp", mt[k][:], mergedT[:, sl].rearrange("(kc p) t -> p kc t", p=128), [mergedT], [mt[k]], sbuf=mt[k])
                c.dma("sp", xt[k][:], x_tm[sl, :], [x_tm], [xt[k]], sbuf=xt[k])
                for nb in range(4):
                    for kc in range(KC):
                        c.op("pe", lambda e: e.matmul(pss[nb][:], lhsT=mt[k][:, kc, :], rhs=Wo[:, kc, nb * 512:(nb + 1) * 512], start=(kc == 0), stop=(kc == KC - 1)), [mt[k], Wo], [pss[nb]])
                    c.op("dve", lambda e: e.scalar_tensor_tensor(out=y1[k][:, nb * 512:(nb + 1) * 512], in0=xt[k][:, nb * 512:(nb + 1) * 512], scalar=DN_ALPHA, in1=pss[nb][:], op0=ALU.mult, op1=ALU.add), [xt[k], pss[nb]], [y1[k]], acc=(nb > 0))
                layernorm(y1[k], hh[k], gam, bet, st[k], mv[k])
                c.dma("pool", h_tm[sl, :], hh[k][:], [hh[k]], [h_tm], sbuf=hh[k], acc=True)
                c.op("act", lambda e: e.copy(out=hb[k][:], in_=hh[k][:]), [hh[k]], [hb[k]])
                for half in range(2):
                    for q in range(8):
                        kc = half * 8 + q
                        c.op("pe", lambda e: e.transpose(out=ps_t[half][:, q * 128:(q + 1) * 128], in_=hb[k][:, kc * 128:(kc + 1) * 128], identity=identb[:]), [hb[k], identb], [ps_t[half]], acc=(q > 0))
                    c.op("act", lambda e: e.copy(out=hTs[k][:, half * 8:(half + 1) * 8, :], in_=ps_t[half][:].rearrange("p (a b) -> p a b", a=8)), [ps_t[half]], [hTs[k]], acc=(half > 0))
                c.dma("pool", hT[:].rearrange("kc p t -> p kc t")[:, :, sl], hTs[k][:], [hTs[k]], [hT], sbuf=hTs[k], acc=True)
            c.end_phase()


        def phase5a():
            c.begin_phase()
            U32 = mybir.dt.uint32
            wst = [c.sb("wst%d" % k, [128, KC, 256], F32) for k in range(2)]
            Wq = c.sb("Wq", [128, KC, D], BF16)
            load_weight_bf16(peer_wq, KC, Wq, wst)
            mc = c.sb("mc", [128, 512], F32)
            c.dma("sp", mc[:], mcst[:], [mcst], [mc], sbuf=mc)
            kst = c.sb("kst", [128, 16, 128], F32)
            kTb = c.sb("kTb", [128, 16, 128], BF16)
            c.dma("sp", kst[:], keysT[:].rearrange("a d k -> d a k"), [keysT], [kst], sbuf=kst)
            c.op("dve", lambda e: e.tensor_copy(out=kTb[:], in_=kst[:]), [kst], [kTb])
            hTb = [c.sb("hTb0", [128, KC, 512], BF16)] * 2
            qT = c.sb("qT", [128, 16, 512], BF16)
            ssb = [c.sb("ssb%d" % k, [128, 16, 128], F32) for k in range(2)]
            t16 = c.sb("t16", [128, 8, 2, 16], F32)
            ix = c.sb("ix", [128, 8, 2, 16], U32)
            ixf = c.sb("ixf", [128, 8, 2, 16], F32)
            wk = c.sb("wk", [128, 128], F32)
            cand = c.sb("cand", [128, 8, 16, 16], F32)
            wk2 = c.sb("wk2", [128, 256], F32)
            b16 = c.sb("b16", [128, 8, 16], F32)
            px = c.sb("px", [128, 8, 16], U32)
            pabf = c.sb("pabf", [128, 2, 8, 16], F32)
            oh = c.sb("oh", [128, 8, 16, 16], F32)
            e16 = c.sb("e16", [128, 8, 16], F32)
            sm = c.sb("sm", [128, 16], F32)
            IJg = [c.sb("IJg%d" % k, [128, 3, 128], F32) for k in range(2)]
            IJgT = [c.sb("IJgT%d" % k, [128, 3, 128], F32) for k in range(2)]
            psq = [c.ps("psq%d" % k, [128, 512], F32) for k in range(2)]
            pss = [c.ps("pss%d" % k, [128, 512], F32) for k in range(2)]
            pst = [c.ps("pst%d" % k, [128, 384], F32) for k in range(2)]
            io16 = mc[:, 384:400]
            identf = mc[:, 0:128]
            ubf = [c.sb("ubf%d" % k, [128, KC, 128], BF16) for k in range(2)]
            vbf = [c.sb("vbf%d" % k, [128, D], BF16) for k in range(2)]
            tj = [0]

            def table_iter():
                j = tj[0]
                if j >= 128:
                    return
                tj[0] += 1
                k = j % 2
                ust = wst[k][:, 0:8, :].rearrange("p a b -> p (a b)").rearrange("p (dc i) -> p dc i", i=128)
                vst = wst[k][:, 8:16, :].rearrange("p a b -> p (a b)")
                c.dma("sp", ust, uT[j].rearrange("dc p i -> p dc i"), [uT], [wst[k]], sbuf=wst[k])
                c.dma("sp", vst, vj[j], [vj], [wst[k]], sbuf=wst[k], par=True)
                c.op("act", lambda e: e.copy(out=ubf[k][:], in_=ust), [wst[k]], [ubf[k]])
                c.op("act", lambda e: e.copy(out=vbf[k][:], in_=vst), [wst[k]], [vbf[k]])
                c.dma("pool", uTb[j], ubf[k][:], [ubf[k]], [uTb], sbuf=ubf[k], acc=True)
                c.dma("pool", vb[j], vbf[k][:], [vbf[k]], [vb], sbuf=vbf[k], acc=True)

            n_tbl = -(-128 // (T // 128))
            it = 0
            for tb in range(T // 512):
                kb = tb % 2
                c.dma("sp", hTb[kb][:], hT[:].rearrange("kc p t -> p kc t")[:, :, tb * 512:(tb + 1) * 512], [hT], [hTb[kb]], sbuf=hTb[kb])
                for hc in range(16):
                    p = psq[hc % 2]
                    for kc in range(KC):
                        c.op("pe", lambda e: e.matmul(p[:], lhsT=Wq[:, kc, hc * 128:(hc + 1) * 128], rhs=hTb[kb][:, kc, :], start=(kc == 0), stop=(kc == KC - 1)), [Wq, hTb[kb]], [p])
                    c.op("act", lambda e: e.copy(out=qT[:, hc, :], in_=p[:]), [p], [qT], acc=True)
                for tt in range(4):
                    k = it % 2
                    it += 1
                    tsl = slice(tt * 128, (tt + 1) * 128)
                    for _ in range(n_tbl):
                        table_iter()
                    for qd in range(4):
                        p = pss[qd % 2]
                        for jj in range(4):
                            hc = qd * 4 + jj
                            c.op("pe", lambda e: e.matmul(p[:, jj * 128:(jj + 1) * 128], lhsT=qT[:, hc, tsl], rhs=kTb[:, hc, :], start=True, stop=True), [qT, kTb], [p], acc=(jj > 0))
                        c.op("act", lambda e: e.copy(out=ssb[k][:, qd * 4:(qd + 1) * 4, :], in_=p[:].rearrange("p (a b) -> p a b", a=4)), [p], [ssb[k]], acc=(qd > 0))
                    X = mybir.AxisListType.X
                    for h in range(8):
                        for cc in range(2):
                            sv = ssb[k][:, 2 * h + cc, :]
                            c.op("dve", lambda e: e.max(out=t16[:, h, cc, 0:8], in_=sv), [ssb[k]], [t16], acc=True)
                            c.op("dve", lambda e: e.max_index(out=ix[:, h, cc, 0:8], in_max=t16[:, h, cc, 0:8], in_values=sv), [ssb[k], t16], [ix], acc=True)
                            c.op("dve", lambda e: e.match_replace(out=wk[:], in_to_replace=t16[:, h, cc, 0:8], in_values=sv, imm_value=-1e30), [ssb[k], t16], [wk])
                            c.op("dve", lambda e: e.max(out=t16[:, h, cc, 8:16], in_=wk[:]), [wk], [t16], acc=True)
                            c.op("dve", lambda e: e.max_index(out=ix[:, h, cc, 8:16], in_max=t16[:, h, cc, 8:16], in_values=wk[:]), [wk, t16], [ix], acc=True)
                    c.op("dve", lambda e: e.tensor_copy(out=ixf[:], in_=ix[:]), [ix], [ixf])
                    c.op("dve", lambda e: e.tensor_tensor(out=cand[:], in0=t16[:, :, 0, :].unsqueeze(3).to_broadcast([128, 8, 16, 16]), in1=t16[:, :, 1, :].unsqueeze(2).to_broadcast([128, 8, 16, 16]), op=ALU.add), [t16], [cand])
                    for h in range(8):
                        cflat = cand[:, h].rearrange("p a b -> p (a b)")
                        c.op("dve", lambda e: e.max(out=b16[:, h, 0:8], in_=cflat), [cand], [b16], acc=True)
                        c.op("dve", lambda e: e.max_index(out=px[:, h, 0:8], in_max=b16[:, h, 0:8], in_values=cflat), [cand, b16], [px], acc=True)
                        c.op("dve", lambda e: e.match_replace(out=wk2[:], in_to_replace=b16[:, h, 0:8], in_values=cflat, imm_value=-1e30), [cand, b16], [wk2])
                        c.op("dve", lambda e: e.max(out=b16[:, h, 8:16], in_=wk2[:]), [wk2], [b16], acc=True)
                        c.op("dve", lambda e: e.max_index(out=px[:, h, 8:16], in_max=b16[:, h, 8:16], in_values=wk2[:]), [wk2, b16], [px], acc=True)
                    c.op("dve", lambda e: e.tensor_copy(out=e16[:], in_=px[:]), [px], [e16])
                    c.op("dve", lambda e: e.tensor_tensor(out=oh[:], in0=e16[:].unsqueeze(3).to_broadcast([128, 8, 16, 16]), in1=mc[:, 400:416].unsqueeze(1).unsqueeze(1).to_broadcast([128, 8, 16, 16]), op=ALU.is_ge), [e16, mc], [oh])
                    c.op("dve", lambda e: e.tensor_reduce(out=pabf[:, 0], in_=oh[:], axis=X, op=ALU.add), [oh], [pabf], acc=True)
                    c.op("dve", lambda e: e.scalar_tensor_tensor(out=pabf[:, 1], in0=pabf[:, 0], scalar=-16.0, in1=e16[:], op0=ALU.mult, op1=ALU.add), [e16, pabf], [pabf], acc=True)
                    for cc in range(2):
                        c.op("dve", lambda e: e.tensor_tensor(out=oh[:], in0=pabf[:, cc].unsqueeze(3).to_broadcast([128, 8, 16, 16]), in1=io16.unsqueeze(1).unsqueeze(1).to_broadcast([128, 8, 16, 16]), op=ALU.is_equal), [pabf, mc], [oh])
                        c.op("dve", lambda e: e.tensor_tensor(out=oh[:], in0=oh[:], in1=ixf[:, :, cc, :].unsqueeze(2).to_broadcast([128, 8, 16, 16]), op=ALU.mult), [oh, ixf], [oh])
                        c.op("dve", lambda e: e.tensor_reduce(out=IJg[k][:, cc, :].rearrange("p (h k) -> p h k", h=8), in_=oh[:], axis=X, op=ALU.add), [oh], [IJg[k]], acc=True)
                    c.op("dve", lambda e: e.tensor_tensor(out=e16[:], in0=b16[:], in1=b16[:, :, 0:1].to_broadcast([128, 8, 16]), op=ALU.subtract), [b16], [e16])
                    c.op("act", lambda e: e.activation(out=e16[:], in_=e16[:], func=AF.Exp), [e16], [e16])
                    c.op("dve", lambda e: e.tensor_reduce(out=sm[:, 0:8], in_=e16[:], axis=X, op=ALU.add), [e16], [sm])
                    c.op("dve", lambda e: e.reciprocal(out=sm[:, 8:16], in_=sm[:, 0:8]), [sm], [sm])
                    c.op("dve", lambda e: e.tensor_tensor(out=IJg[k][:, 2, :].rearrange("p (h k) -> p h k", h=8), in0=e16[:], in1=sm[:, 8:16].unsqueeze(2).to_broadcast([128, 8, 16]), op=ALU.mult), [e16, sm], [IJg[k]], acc=True)
                    for q in range(3):
                        c.op("pe", lambda e: e.transpose(out=pst[k][:, q * 128:(q + 1) * 128], in_=IJg[k][:, q, :], identity=identf), [IJg[k], mc], [pst[k]], acc=(q > 0))
                    c.op("act", lambda e: e.copy(out=IJgT[k][:], in_=pst[k][:].rearrange("p (a b) -> p a b", a=3)), [pst[k]], [IJgT[k]])
                    t0 = tb * 512 + tt * 128
                    c.dma("pool", IT[:, t0:t0 + 128], IJgT[k][:, 0, :], [IJgT[k]], [IT], sbuf=IJgT[k], acc=True)
                    c.dma("pool", JT[:, t0:t0 + 128], IJgT[k][:, 1, :], [IJgT[k]], [JT], sbuf=IJgT[k], acc=True)
                    c.dma("pool", gT[:, t0:t0 + 128], IJgT[k][:, 2, :], [IJgT[k]], [gT], sbuf=IJgT[k], acc=True)
            c.end_phase()

        def phase5b():
            c.begin_phase()
            wst = [c.sb("wst%d" % k, [128, KC, 256], F32) for k in range(2)]
            Wpg = c.sb("Wpg", [128, KC, D], BF16)
            Wpe = c.sb("Wpe", [128, 2, D], BF16)
            load_weight_bf16(w_pg, KC, Wpg, wst)
            load_weight_bf16(w_pe, 2, Wpe, wst)
            hTt = [c.sb("hTt%d" % k, [128, KC, 128], BF16) for k in range(2)]
            pTt = [c.sb("pTt%d" % k, [128, 2, 128], BF16) for k in range(2)]
            ht = [c.sb("ht%d" % k, [128, D], F32) for k in range(2)]
            bs = [c.sb("bs%d" % k, [128, D], F32) for k in range(2)]
            sg = [c.sb("sg%d" % k, [128, 512], F32) for k in range(2)]
            pe_ = [c.sb("pe_%d" % k, [128, 512], F32) for k in range(2)]
            ps1 = [c.ps("ps1%d" % k, [128, 512], F32) for k in range(2)]
            ps2 = [c.ps("ps2%d" % k, [128, 512], F32) for k in range(2)]
            it = 0
            for n in range(NT):
                k = n % 2
                sl = slice(n * 128, (n + 1) * 128)
                c.dma("sp", hTt[k][:], hT[:].rearrange("kc p t -> p kc t")[:, :, sl], [hT], [hTt[k]], sbuf=hTt[k])
                c.dma("sp", pTt[k][:], pb[:].rearrange("kc p t -> p kc t")[:, :, sl], [pb], [pTt[k]], sbuf=pTt[k])
                c.dma("sp", ht[k][:], h_tm[sl, :], [h_tm], [ht[k]], sbuf=ht[k])
                for nb in range(4):
                    q = it % 2
                    it += 1
                    ns = slice(nb * 512, (nb + 1) * 512)
                    for kc in range(KC):
                        c.op("pe", lambda e: e.matmul(ps1[q][:], lhsT=hTt[k][:, kc, :], rhs=Wpg[:, kc, ns], start=(kc == 0), stop=(kc == KC - 1)), [hTt[k], Wpg], [ps1[q]])
                    for kc in range(2):
                        c.op("pe", lambda e: e.matmul(ps2[q][:], lhsT=pTt[k][:, kc, :], rhs=Wpe[:, kc, ns], start=(kc == 0), stop=(kc == 1)), [pTt[k], Wpe], [ps2[q]])
                    c.op("act", lambda e: e.activation(out=sg[q][:], in_=ps1[q][:], func=AF.Sigmoid), [ps1[q]], [sg[q]])
                    c.op("dve", lambda e: e.tensor_tensor(out=pe_[q][:], in0=ps2[q][:], in1=sg[q][:], op=ALU.mult), [ps2[q], sg[q]], [pe_[q]])
                    c.op("dve", lambda e: e.scalar_tensor_tensor(out=bs[k][:, ns], in0=ht[k][:, ns], scalar=DN_ALPHA, in1=pe_[q][:], op0=ALU.mult, op1=ALU.add), [ht[k], pe_[q]], [bs[k]], acc=(nb > 0))
                c.dma("pool", base[sl, :], bs[k][:], [bs[k]], [base], sbuf=bs[k], acc=True)
            c.end_phase()

        def phase5c():
            c.begin_phase()
            ust = [c.sb("ust%d" % k, [128, KC, 128], F32) for k in range(2)]
            ubf = [c.sb("ubf%d" % k, [128, KC, 128], BF16) for k in range(2)]
            vst = [c.sb("vst%d" % k, [128, D], F32) for k in range(2)]
            vbf = [c.sb("vbf%d" % k, [128, D], BF16) for k in range(2)]
            for j in range(128):
                k = j % 2
                c.dma("sp", ust[k][:], uT[j].rearrange("dc p i -> p dc i"), [uT], [ust[k]], sbuf=ust[k])
                c.op("act", lambda e: e.copy(out=ubf[k][:], in_=ust[k][:]), [ust[k]], [ubf[k]])
                c.dma("pool", uTb[j], ubf[k][:], [ubf[k]], [uTb], sbuf=ubf[k], acc=True)
                c.dma("sp", vst[k][:], vj[j], [vj], [vst[k]], sbuf=vst[k])
                c.op("dve", lambda e: e.tensor_copy(out=vbf[k][:, 0:1024], in_=vst[k][:, 0:1024]), [vst[k]], [vbf[k]])
                c.op("pool", lambda e: e.tensor_copy(out=vbf[k][:, 1024:D], in_=vst[k][:, 1024:D]), [vst[k]], [vbf[k]], acc=True)
                c.dma("pool", vb[j], vbf[k][:], [vbf[k]], [vb], sbuf=vbf[k], acc=True)
            c.end_phase()

        def phase5d():
            c.begin_phase()
            TB = 256
            mc = c.sb("mc", [128, 512], F32)
            c.dma("sp", mc[:], mcst[:], [mcst], [mc], sbuf=mc)
            iotab = c.sb("iotab", [128, 128], BF16)
            c.op("dve", lambda e: e.tensor_copy(out=iotab[:], in_=mc[:, 256:384]), [mc], [iotab])
            iota = iotab[:]
            gam, bet = c.sb("gam", [128, D], F32), c.sb("bet", [128, D], F32)
            c.dma("sp", gam[:], ln2_g[:].rearrange("a b -> (a b)").partition_broadcast(128), [ln2_g], [gam], sbuf=gam)
            c.dma("sp", bet[:], ln2_b[:].rearrange("a b -> (a b)").partition_broadcast(128), [ln2_b], [bet], sbuf=bet)
            A = c.sb("A", [128, 128, TB], BF16)
            hTb = [c.sb("hTb%d" % k, [128, KC, TB], BF16) for k in range(2)]
            itj = [c.sb("itj%d" % k, [128, 3, TB], F32) for k in range(2)]
            itjb = [c.sb("itjb%d" % k, [128, 3, TB], BF16) for k in range(2)]
            NPF = 5
            TBW = 16
            ub = [c.sb("ub%d" % k, [128, KC, 128], BF16) for k in range(NPF)]
            vbt = [c.sb("vbt%d" % k, [128, D], BF16) for k in range(NPF)]
            OI = [c.sb("OI%d" % k, [128, TBW, 128], BF16) for k in range(2)]
            OJ = [c.sb("OJ%d" % k, [128, TBW, 128], BF16) for k in range(2)]
            bs = [c.sb("bs0", [128, D], F32)] * 2
            yo = [c.sb("yo0", [128, D], F32)] * 2
            yn = yo
            st = [c.sb("st%d" % k, [128, 24], F32) for k in range(2)]
            mv = [c.sb("mv%d" % k, [128, 4], F32) for k in range(2)]
            pbk = [c.ps("pbk%d" % k, [128, 512], F32) for k in range(8)]
            for blk in range(T // TB):
                kb = blk % 2
                bsl = slice(blk * TB, (blk + 1) * TB)
                c.dma("sp", hTb[kb][:], hT[:].rearrange("kc p t -> p kc t")[:, :, bsl], [hT], [hTb[kb]], sbuf=hTb[kb])
                c.dma("sp", itj[kb][:, 0, :], IT[:, bsl], [IT], [itj[kb]], sbuf=itj[kb])
                c.dma("sp", itj[kb][:, 1, :], JT[:, bsl], [JT], [itj[kb]], sbuf=itj[kb], acc=True)
                c.dma("sp", itj[kb][:, 2, :], gT[:, bsl], [gT], [itj[kb]], sbuf=itj[kb], acc=True)
                for j in range(128):
                    ku = j % NPF
                    p = pbk[j % 2]
                    c.dma("sp", ub[ku][:], uTb[j], [uTb], [ub[ku]], sbuf=ub[ku])
                    for kc in range(KC):
                        c.op("pe", lambda e: e.matmul(p[:, 0:TB], lhsT=ub[ku][:, kc, :], rhs=hTb[kb][:, kc, :], start=(kc == 0), stop=(kc == KC - 1)), [ub[ku], hTb[kb]], [p])
                    c.op("act", lambda e: e.activation(out=A[:, j, :], in_=p[:, 0:TB], func=AF.Gelu), [p], [A], acc=True)
                c.op("dve", lambda e: e.tensor_copy(out=itjb[kb][:], in_=itj[kb][:]), [itj[kb]], [itjb[kb]])
                for tb0 in range(0, TB, TBW):
                    kq = (tb0 // TBW) % 2
                    io_b = iota.unsqueeze(1).to_broadcast([128, TBW, 128])
                    c.op("dve", lambda e: e.tensor_tensor(out=OI[kq][:], in0=io_b, in1=itjb[kb][:, 0, tb0:tb0 + TBW].unsqueeze(2).to_broadcast([128, TBW, 128]), op=ALU.is_equal), [iotab, itjb[kb]], [OI[kq]])
                    c.op("dve", lambda e: e.tensor_tensor(out=OI[kq][:], in0=OI[kq][:], in1=itjb[kb][:, 2, tb0:tb0 + TBW].unsqueeze(2).to_broadcast([128, TBW, 128]), op=ALU.mult), [OI[kq], itjb[kb]], [OI[kq]])
                    c.op("dve", lambda e: e.tensor_tensor(out=OJ[kq][:], in0=io_b, in1=itjb[kb][:, 1, tb0:tb0 + TBW].unsqueeze(2).to_broadcast([128, TBW, 128]), op=ALU.is_equal), [iotab, itjb[kb]], [OJ[kq]])
                    for t4 in range(0, TBW, 4):
                        pw = pbk[2 + (t4 // 4) % 2]
                        for q in range(4):
                            c.op("pe", lambda e: e.matmul(pw[:, q * 128:(q + 1) * 128], lhsT=OI[kq][:, t4 + q, :], rhs=OJ[kq][:, t4 + q, :], start=True, stop=True), [OI[kq], OJ[kq]], [pw], acc=(q > 0))
                        t = tb0 + t4
                        c.op("dve", lambda e: e.tensor_tensor(out=A[:, :, t:t + 4], in0=pw[:].rearrange("p (q j) -> p j q", q=4), in1=A[:, :, t:t + 4], op=ALU.mult), [pw, A], [A], acc=True)
                for j in range(128):
                    kv = j % NPF
                    c.dma("sp", vbt[kv][:], vb[j], [vb], [vbt[kv]], sbuf=vbt[kv])
                    for tt in range(2):
                        for nb in range(4):
                            c.op("pe", lambda e: e.matmul(pbk[tt * 4 + nb][:], lhsT=A[:, j, tt * 128:(tt + 1) * 128], rhs=vbt[kv][:, nb * 512:(nb + 1) * 512], start=(j == 0), stop=(j == 127)), [A, vbt[kv]], [pbk[tt * 4 + nb]])
                for tt in range(2):
                    k = tt
                    rs_ = slice(blk * TB + tt * 128, blk * TB + (tt + 1) * 128)
                    c.dma("sp", bs[k][:], base[rs_, :], [base], [bs[k]], sbuf=bs[k])
                    for nb in range(4):
                        ns = slice(nb * 512, (nb + 1) * 512)
                        c.op("dve", lambda e: e.tensor_tensor(out=yo[k][:, ns], in0=pbk[tt * 4 + nb][:], in1=bs[k][:, ns], op=ALU.add), [pbk[tt * 4 + nb], bs[k]], [yo[k]], acc=(nb > 0))
                    layernorm(yo[k], yn[k], gam, bet, st[k], mv[k])
                    c.dma("pool", y[rs_, :], yn[k][:], [yn[k]], [y], sbuf=yn[k], acc=True)
            c.end_phase()

        phases = [phase1, phaseA, phase2, phase3, phase4a, phase4b, phase5a, phase5b, phase5d]
        for i, ph in enumerate(phases):
            if i < upto:
                ph()
        c.barrier()
    return nc


def shared_inputs(T, inp):
    sh = {
        "rcst": ret_consts(T), "mcst": misc_consts(),
        "w_in": np.ascontiguousarray(inp["w_in"][0]),
        "ret_decay_logit": np.ascontiguousarray(inp["ret_decay_logit"][0]),
        "w_ret_o": np.ascontiguousarray(inp["w_ret_o"][0]), "w_att_o": np.ascontiguousarray(inp["w_att_o"][0]),
        "w_out": np.ascontiguousarray(inp["w_out"][0]), "rel_bias": np.ascontiguousarray(inp["rel_bias"]),
        "ln1_g": np.ascontiguousarray(inp["ln1_g"]), "ln1_b": np.ascontiguousarray(inp["ln1_b"]),
        "ln2_g": np.ascontiguousarray(inp["ln2_g"]), "ln2_b": np.ascontiguousarray(inp["ln2_b"]),
        "peer_wq": np.ascontiguousarray(inp["peer_wq"][0]),
        "keysT": np.ascontiguousarray(inp["peer_keys"][0].reshape(16, 128, 128).transpose(0, 2, 1)),
        "uT": np.ascontiguousarray(inp["peer_u"][0].reshape(128, 128, KC, 128).transpose(1, 2, 3, 0)),
        "vj": np.ascontiguousarray(inp["peer_v"][0].reshape(128, 128, D).transpose(1, 0, 2)),
        "w_pe": np.ascontiguousarray(inp["w_pe"][0]), "w_pg": np.ascontiguousarray(inp["w_pg"][0]),
    }
    for g in range(3):
        sh["am%d" % g] = ATT_TABLES[g][1]
    return sh


def core_inputs(x_seq, p_seq, s0, T, sh):
    S = x_seq.shape[0]
    has_l, has_r = s0 > 0, s0 + T < S
    ext = np.zeros((T + 2 * HALO, D), np.float32)
    ext[HALO:HALO + T] = x_seq[s0:s0 + T]
    nbr = np.zeros((T, D), np.float32)
    pos_n = np.arange(T)
    if has_l:
        ext[:HALO] = x_seq[s0 - HALO:s0]
        nbr = x_seq[s0 - T:s0]
        pos_n = np.arange(s0 - T, s0)
    if has_r:
        ext[HALO + T:] = x_seq[s0 + T:s0 + T + HALO]
        nbr = x_seq[s0 + T:s0 + 2 * T]
        pos_n = np.arange(s0 + T, s0 + 2 * T)
    fl = np.zeros((128, 4), np.float32)
    fl[:, 0], fl[:, 1] = float(has_l), float(has_r)
    fl[:, 2], fl[:, 3] = 1.0, 1.0
    fl[:64, 2] = float(has_l)
    fl[64:, 3] = float(has_r)
    m = dict(sh)
    m.update({
        "xT_ext": np.ascontiguousarray(ext.T), "xT_nbr": np.ascontiguousarray(nbr.T),
        "x_tm": np.ascontiguousarray(x_seq[s0:s0 + T]), "pT": np.ascontiguousarray(p_seq[s0:s0 + T].T),
        "flags": fl, "cs_own": rot_table(np.arange(s0, s0 + T)), "cs_nbr": rot_table(pos_n),
    })
    return m


_NC_CACHE = {}


def kernel(**inputs):
    T = 4096
    inp = {k: np.asarray(v) for k, v in inputs.items()}
    sh = shared_inputs(T, inp)
    in_maps = []
    for b in range(4):
        in_maps.append(core_inputs(inp["x_prompt"][b], inp["p_prompt"][0, b], 0, T, sh))
    for b in range(2):
        for hf in range(2):
            in_maps.append(core_inputs(inp["x_sample"][b], inp["p_sample"][0, b], hf * T, T, sh))
    if T not in _NC_CACHE:
        _NC_CACHE[T] = build(T)
    res = run_bass_kernel_spmd(_NC_CACHE[T], in_maps, core_ids=list(range(8)))
    ys = [np.asarray(r["y"], np.float32) for r in res.results]
    y_prompt = np.stack(ys[0:4], 0)
    y_sample = np.stack([np.concatenate(ys[4:6], 0), np.concatenate(ys[6:8], 0)], 0)
    return (y_prompt, y_sample)
```
